# Optimizing a Trainium2 kernel written in Bass

```python
import jax, jax.numpy as jnp
from jax import lax
import numpy as np

D_MODEL = 1024
BATCH = 8
SEQ = 4096
DEPTH = 2

CTX_LEN = 256
GRID_W = 64
HEAD_DIM = 64
AXIS_DIM = HEAD_DIM // 2
ATTN_WIDTH = D_MODEL // 2
N_Q_HEADS = ATTN_WIDTH // HEAD_DIM
N_KV_HEADS = N_Q_HEADS // 4
KV_WIDTH = N_KV_HEADS * HEAD_DIM
FOURIER_WIDTH = D_MODEL // 4
FOURIER_GROUP = 64
N_FOURIER_GROUPS = FOURIER_WIDTH // FOURIER_GROUP
GMLP_WIDTH = D_MODEL // 4
GMLP_HEAD = 64
N_GMLP_HEADS = GMLP_WIDTH // GMLP_HEAD
CHUNK = 128
Q_BLOCK = 128
MIX_WIDTH = ATTN_WIDTH + FOURIER_WIDTH + GMLP_WIDTH
IN_WIDTH = ATTN_WIDTH + 2 * KV_WIDTH + FOURIER_WIDTH + 2 * GMLP_WIDTH
SPLITS = [ATTN_WIDTH, ATTN_WIDTH + KV_WIDTH, ATTN_WIDTH + 2 * KV_WIDTH,
          ATTN_WIDTH + 2 * KV_WIDTH + FOURIER_WIDTH,
          ATTN_WIDTH + 2 * KV_WIDTH + FOURIER_WIDTH + GMLP_WIDTH]
D_FF = 4 * D_MODEL
ROPE_THETA = 10000.0
EPS = 1e-6

kernel_name = 'hybrid_fourier_gmlp_gqa_prefix_dit'


def rms_norm(x, g):
    xf = x.astype(jnp.float32)
    y = xf * lax.rsqrt(jnp.mean(xf * xf, axis=-1, keepdims=True) + EPS)
    return (y * g.astype(jnp.float32)).astype(x.dtype)


def modulate(h, shift, scale):
    return h * (1 + scale) + shift


def adaln(cond, w, b):
    return jnp.split(jax.nn.silu(cond) @ w + b, 6, axis=-1)


def axial_rope_tables(n, dtype):
    rows = n // GRID_W
    row = jnp.repeat(jnp.arange(rows, dtype=jnp.float32), GRID_W)
    col = jnp.tile(jnp.arange(GRID_W, dtype=jnp.float32), rows)
    inv = ROPE_THETA ** (-jnp.arange(0, AXIS_DIM, 2, dtype=jnp.float32) / AXIS_DIM)
    ar = (row[:, None] * inv)[:, None, :]
    ac = (col[:, None] * inv)[:, None, :]
    return tuple(t.astype(dtype) for t in (jnp.cos(ar), jnp.sin(ar), jnp.cos(ac), jnp.sin(ac)))


def rope_1d(x, cos, sin):
    x1, x2 = jnp.split(x, 2, axis=-1)
    return jnp.concatenate([x1 * cos - x2 * sin, x1 * sin + x2 * cos], axis=-1)


def apply_axial_rope(x, tabs):
    cr, sr, cc, sc = tabs
    xr, xc = jnp.split(x, 2, axis=-1)
    return jnp.concatenate([rope_1d(xr, cr, sr), rope_1d(xc, cc, sc)], axis=-1)


def heads(t, h):
    return t.reshape(t.shape[0], t.shape[1], h, HEAD_DIM)


def attend(q, k, v):
    s = jnp.einsum('bqkgd,btkd->bkgqt', q, k).astype(jnp.float32) * (HEAD_DIM ** -0.5)
    p = jax.nn.softmax(s, axis=-1).astype(v.dtype)
    return jnp.einsum('bkgqt,btkd->bqkgd', p, v)


def latent_attention(q, k, v):
    b, s, hq, d = q.shape
    g = hq // N_KV_HEADS
    nblk = s // Q_BLOCK
    qb = q.reshape(b, nblk, Q_BLOCK, N_KV_HEADS, g, d).transpose(1, 0, 2, 3, 4, 5)
    out = lax.map(lambda qblk: attend(qblk, k, v), qb)
    return out.transpose(1, 0, 2, 3, 4, 5).reshape(b, s, hq * d)


def context_attention(q, k, v):
    b, n, hq, d = q.shape
    g = hq // N_KV_HEADS
    return attend(q.reshape(b, n, N_KV_HEADS, g, d), k, v).reshape(b, n, hq * d)


def fourier_mix(f):
    b, n, _ = f.shape
    fg = f.astype(jnp.float32).reshape(b, n, N_FOURIER_GROUPS, FOURIER_GROUP)
    y = jnp.fft.fft2(fg, axes=(1, 3), norm='ortho').real
    return y.reshape(b, n, FOURIER_WIDTH).astype(f.dtype)


def chunk_spatial_gate(u, v, v_g, w_s, b_s):
    b, n, _ = u.shape
    u = jax.nn.gelu(u)
    vh = jax.nn.gelu(v).reshape(b, n // CHUNK, CHUNK, N_GMLP_HEADS, GMLP_HEAD)
    vh = rms_norm(vh, v_g.reshape(N_GMLP_HEADS, GMLP_HEAD))
    sp = jnp.einsum('hpq,bcqhd->bcphd', w_s, vh) + b_s.T[None, None, :, :, None]
    return u * sp.reshape(b, n, GMLP_WIDTH)


def squared_relu_mlp(h, w1, w2):
    return jnp.square(jax.nn.relu(h @ w1)) @ w2


def layer(x, ctx, c, c_ctx, w_ada, b_ada, g1, g2, w_in, q_g, k_g, v_g, w_s, b_s, w_out, w1, w2, update_ctx):
    sh1, sc1, gt1, sh2, sc2, gt2 = [m[:, None, :] for m in adaln(c, w_ada, b_ada)]
    csh1, csc1, cgt1, csh2, csc2, cgt2 = adaln(c_ctx, w_ada, b_ada)

    px = modulate(rms_norm(x, g1), sh1, sc1) @ w_in
    pc = modulate(rms_norm(ctx, g1), csh1, csc1) @ w_in
    qx, kx, vx, fx, ux, gvx = jnp.split(px, SPLITS, axis=-1)
    qc, kc, vc, fc, uc, gvc = jnp.split(pc, SPLITS, axis=-1)

    tabs = axial_rope_tables(x.shape[1], x.dtype)
    qx = apply_axial_rope(rms_norm(heads(qx, N_Q_HEADS), q_g), tabs)
    kx = apply_axial_rope(rms_norm(heads(kx, N_KV_HEADS), k_g), tabs)
    kc = rms_norm(heads(kc, N_KV_HEADS), k_g)
    vx = heads(vx, N_KV_HEADS)
    vc = heads(vc, N_KV_HEADS)
    k_all = jnp.concatenate([kc, kx], axis=1)
    v_all = jnp.concatenate([vc, vx], axis=1)

    mix_x = jnp.concatenate([latent_attention(qx, k_all, v_all),
                             fourier_mix(fx),
                             chunk_spatial_gate(ux, gvx, v_g, w_s, b_s)], axis=-1) @ w_out
    x = x + gt1 * mix_x
    x = x + gt2 * squared_relu_mlp(modulate(rms_norm(x, g2), sh2, sc2), w1, w2)

    if update_ctx:
        qc = rms_norm(heads(qc, N_Q_HEADS), q_g)
        mix_c = jnp.concatenate([context_attention(qc, kc, vc),
                                 fourier_mix(fc),
                                 chunk_spatial_gate(uc, gvc, v_g, w_s, b_s)], axis=-1) @ w_out
        ctx = ctx + cgt1 * mix_c
        ctx = ctx + cgt2 * squared_relu_mlp(modulate(rms_norm(ctx, g2), csh2, csc2), w1, w2)
    return x, ctx


def setup_inputs(seed: int = 0) -> dict:
    key = jax.random.key(seed)
    ks = jax.random.split(key, 17)
    f32 = jnp.float32
    n = lambda k, s: jax.random.normal(k, s, dtype=f32)
    return {
        'x': n(ks[0], (BATCH, SEQ, D_MODEL)),
        'c': n(ks[1], (BATCH, D_MODEL)),
        'ctx': n(ks[2], (BATCH, CTX_LEN, D_MODEL)),
        'c_ctx': n(ks[3], (D_MODEL,)),
        'w_ada': n(ks[4], (DEPTH, D_MODEL, 6 * D_MODEL)) * (0.5 * D_MODEL ** -0.5),
        'b_ada': n(ks[5], (DEPTH, 6 * D_MODEL)) * 0.01,
        'norm1_g': 1.0 + 0.02 * n(ks[6], (DEPTH, D_MODEL)),
        'norm2_g': 1.0 + 0.02 * n(ks[7], (DEPTH, D_MODEL)),
        'w_in': n(ks[8], (DEPTH, D_MODEL, IN_WIDTH)) * D_MODEL ** -0.5,
        'q_norm_g': 1.0 + 0.02 * n(ks[9], (DEPTH, HEAD_DIM)),
        'k_norm_g': 1.0 + 0.02 * n(ks[10], (DEPTH, HEAD_DIM)),
        'gmlp_v_g': 1.0 + 0.02 * n(ks[11], (DEPTH, GMLP_WIDTH)),
        'w_spatial': n(ks[12], (DEPTH, N_GMLP_HEADS, CHUNK, CHUNK)) * CHUNK ** -0.5,
        'b_spatial': n(ks[13], (DEPTH, N_GMLP_HEADS, CHUNK)) * 0.02,
        'w_out': n(ks[14], (DEPTH, MIX_WIDTH, D_MODEL)) * MIX_WIDTH ** -0.5,
        'w_mlp1': n(ks[15], (DEPTH, D_MODEL, D_FF)) * D_MODEL ** -0.5,
        'w_mlp2': n(ks[16], (DEPTH, D_FF, D_MODEL)) * D_FF ** -0.5,
    }


def reference(x, c, ctx, c_ctx, w_ada, b_ada, norm1_g, norm2_g, w_in, q_norm_g, k_norm_g, gmlp_v_g,
              w_spatial, b_spatial, w_out, w_mlp1, w_mlp2):
    for l in range(DEPTH):
        x, ctx = layer(x, ctx, c, c_ctx, w_ada[l], b_ada[l], norm1_g[l], norm2_g[l], w_in[l],
                       q_norm_g[l], k_norm_g[l], gmlp_v_g[l], w_spatial[l], b_spatial[l], w_out[l],
                       w_mlp1[l], w_mlp2[l], l < DEPTH - 1)
    return x
```

```python
import bisect
from contextlib import ExitStack

import numpy as np
import ml_dtypes

import concourse.bass as bass
import concourse.mybir as mybir
from concourse.bass_utils import run_bass_kernel_spmd

F32 = mybir.dt.float32
BF16 = mybir.dt.bfloat16
AF = mybir.ActivationFunctionType
ALU = mybir.AluOpType
AX = mybir.AxisListType

DEPTH = 2
D = 1024
S_LAT = 4096
CTX = 256
NTOK = S_LAT + CTX
EPS = 1e-6
NKT = NTOK // 128


class _Stop(Exception):
    pass


class Buf:
    __slots__ = ("name", "w", "r", "sem", "ndma", "nobar", "excl")

    def __init__(self, name):
        self.name = name
        self.nobar = False
        self.excl = False
        self.w = []
        self.r = []
        self.sem = None
        self.ndma = 0


class Eng:
    def __init__(self, name, sem):
        self.name = name
        self.sem = sem
        self.seq = 0
        self.sigseqs = []
        self.waited = {}
        self.prog = []


class Sched:
    def __init__(self, nc, es):
        self.nc = nc
        self.es = es
        self.E = {}
        for n in ("pe", "act", "dve", "pool", "sp"):
            self.E[n] = Eng(n, es.enter_context(nc.semaphore("e_" + n)))
        self.dbufs = []
        self.nsem = 5

    def _resolve(self, tok):
        if tok[0] == "c":
            e, seq = tok[1], tok[2]
            i = bisect.bisect_left(e.sigseqs, seq)
            assert i < len(e.sigseqs), "dependency on unsignaled op on " + e.name
            return e.sem, i + 1, e
        b = tok[1]
        return b.sem, 16 * b.ndma, None

    def _deps(self, eng, reads, writes):
        toks = []
        for b in reads:
            toks.extend(b.w)
        for b in writes:
            toks.extend(b.w)
            toks.extend(b.r)
        need = {}
        for t in toks:
            if t[0] == "c" and t[1] is eng and eng.name == "pe":
                continue
            sem, val, src = self._resolve(t)
            key = id(sem)
            if val > eng.waited.get(key, 0):
                if key not in need or need[key][1] < val:
                    need[key] = (sem, val)
        for key, (sem, val) in need.items():
            eng.prog.append(("wait", sem, val))
            eng.waited[key] = val

    def _mark(self, tok, reads, writes):
        wset = set(id(b) for b in writes)
        for b in reads:
            if id(b) in wset:
                continue
            if tok[0] == "c":
                b.r = [t for t in b.r if not (t[0] == "c" and t[1] is tok[1])]
            elif tok in b.r:
                continue
            b.r.append(tok)
        for b in writes:
            b.w = [tok]
            b.r = []

    def op(self, en, fn, reads, writes, sig=True):
        eng = self.E[en]
        if any(b.excl for b in reads):
            writes = list(writes) + [b for b in reads if b.excl]
            reads = [b for b in reads if not b.excl]
        self._deps(eng, reads, writes)
        eng.seq += 1
        if sig:
            eng.sigseqs.append(eng.seq)
        eng.prog.append(("ins", fn, sig))
        self._mark(("c", eng, eng.seq), reads, writes)

    def dma(self, q, out, in_, reads, writes, semb):
        eng = self.E[q]
        self._deps(eng, reads, writes)
        if semb.sem is None:
            semb.sem = self.es.enter_context(self.nc.semaphore("d_" + semb.name))
            self.nsem += 1
            self.dbufs.append(semb)
        sem = semb.sem
        eng.prog.append(("dma", out, in_, sem))
        semb.ndma += 1
        self._mark(("d", semb), reads, writes)

    def barrier(self):
        targets = []
        for e in self.E.values():
            if e.sigseqs:
                targets.append((e.sem, len(e.sigseqs)))
        for b in self.dbufs:
            if not b.nobar:
                targets.append((b.sem, 16 * b.ndma))
        for e in self.E.values():
            for sem, val in targets:
                key = id(sem)
                if val > e.waited.get(key, 0):
                    e.prog.append(("wait", sem, val))
                    e.waited[key] = val

    def emit(self):
        nc = self.nc
        attr = {"pe": "tensor", "act": "scalar", "dve": "vector", "pool": "gpsimd", "sp": "sync"}
        with nc.Block() as block:
            for n, eng in self.E.items():
                def body(e, eng=eng):
                    for it in eng.prog:
                        if it[0] == "wait":
                            e.wait_ge(it[1], it[2])
                        elif it[0] == "ins":
                            ins = it[1](e)
                            if it[2]:
                                ins.then_inc(eng.sem, 1)
                        else:
                            e.dma_start(out=it[1], in_=it[2]).then_inc(it[3], 16)
                getattr(block, attr[n])(body)


def _bf(a):
    return np.ascontiguousarray(np.asarray(a, dtype=np.float32).astype(ml_dtypes.bfloat16))


_CONST_CACHE = {}


def host_consts():
    if _CONST_CACHE:
        return _CONST_CACHE
    c = {}
    c["ones"] = _bf(np.ones((128, 128)))
    blk = np.zeros((128, 128), np.float32)
    blk[:64, :64] = 1.0
    blk[64:, 64:] = 1.0
    c["blk64"] = _bf(blk)
    perm = np.zeros((128, 128), np.float32)
    for m in range(128):
        d = m % 64
        pd = d + 16 if (d % 32) < 16 else d - 16
        perm[(m // 64) * 64 + pd, m] = 1.0
    c["perm"] = _bf(perm)
    inv = 10000.0 ** (-np.arange(0, 32, 2, dtype=np.float64) / 32.0)
    t = np.arange(S_LAT)
    row = (t // 64).astype(np.float64)
    col = (t % 64).astype(np.float64)
    cosT = np.zeros((128, S_LAT), np.float64)
    sinT = np.zeros((128, S_LAT), np.float64)
    for p in range(128):
        d = p % 64
        pos = row if d < 32 else col
        f = inv[d % 16]
        sgn = -1.0 if (d % 32) < 16 else 1.0
        cosT[p] = np.cos(pos * f)
        sinT[p] = sgn * np.sin(pos * f)
    c["cosT"] = _bf(cosT)
    c["sinT"] = _bf(sinT)
    k = np.arange(64)
    ang = 2 * np.pi * np.outer(k, k) / 64.0
    cc, sc = np.cos(ang), np.sin(ang)
    def bd(m):
        o = np.zeros((128, 128))
        o[:64, :64] = m
        o[64:, 64:] = m
        return o
    c["bdc_lat"] = _bf(bd(cc) / 512.0)
    c["bds_lat"] = _bf(-bd(sc) / 512.0)
    c["bdc_ctx"] = _bf(bd(cc) / 128.0)
    c["bds_ctx"] = _bf(-bd(sc) / 128.0)
    n = np.arange(S_LAT)
    m = (np.outer(n, n) % S_LAT).astype(np.float64)
    def relay(a):
        return np.ascontiguousarray(a.reshape(32, 128, 8, 512).transpose(2, 1, 0, 3))
    c["dftc"] = relay(_bf(np.cos(2 * np.pi * m / S_LAT)))
    c["dfts"] = relay(_bf(np.sin(2 * np.pi * m / S_LAT)))
    n2 = np.arange(CTX)
    m2 = (np.outer(n2, n2) % CTX).astype(np.float64)
    c["dftc256"] = _bf(np.cos(2 * np.pi * m2 / CTX))
    c["dfts256"] = _bf(np.sin(2 * np.pi * m2 / CTX))
    _CONST_CACHE.update(c)
    return c


def host_layout(inp):
    f32 = np.float32
    sh = {}
    L = DEPTH
    sh["w_ada"] = np.ascontiguousarray(inp["w_ada"].reshape(L, 8, 128, 24, 256).transpose(0, 3, 2, 1, 4), f32)
    sh["eye2"] = np.eye(2, dtype=f32)
    sh["b_adaT"] = np.ascontiguousarray(inp["b_ada"].reshape(L, 48, 128).transpose(0, 2, 1), f32)
    sh["g1T"] = np.ascontiguousarray(inp["norm1_g"].reshape(L, 8, 128).transpose(0, 2, 1), f32)
    sh["g2T"] = np.ascontiguousarray(inp["norm2_g"].reshape(L, 8, 128).transpose(0, 2, 1), f32)
    qcols = []
    for j in range(4):
        qcols += list(range(64 * j, 64 * j + 64)) + list(range(64 * (j + 4), 64 * (j + 4) + 64))
    kcols = list(range(512, 640))
    vcols = list(range(640, 768))
    fcols = list(range(768, 1024))
    ucols = list(range(1024, 1280))
    gvcols = list(range(1280, 1536))
    order = qcols + kcols + ucols + vcols + fcols + gvcols
    sh["w_in"] = np.ascontiguousarray(inp["w_in"][:, :, order], f32)
    gq = inp["q_norm_g"]
    gk = inp["k_norm_g"]
    gqk = np.zeros((L, 128, 2), f32)
    gqk[:, :, 0] = np.concatenate([gq, gq], axis=1)
    gqk[:, :, 1] = np.concatenate([gk, gk], axis=1)
    sh["gqk"] = gqk
    sh["vgb"] = np.ascontiguousarray(np.broadcast_to(inp["gmlp_v_g"][:, None, :], (L, 128, 256)), f32)
    sh["w_sT"] = np.ascontiguousarray(inp["w_spatial"].transpose(0, 3, 1, 2), f32)
    bs = inp["b_spatial"].reshape(L, 2, 2, 128)
    bsb = np.broadcast_to(bs.transpose(0, 2, 1, 3)[:, :, None, :, :], (L, 2, 64, 2, 128))
    sh["bsb"] = np.ascontiguousarray(bsb.reshape(L, 128, 2, 128), f32)
    rorder = qcols + list(range(512, 1024))
    sh["w_out"] = np.ascontiguousarray(inp["w_out"][:, rorder, :], f32)
    sh["w1r"] = np.ascontiguousarray(inp["w_mlp1"].reshape(L, 8, 128, 8, 512).transpose(0, 3, 2, 1, 4), f32)
    sh["w2r"] = np.ascontiguousarray(inp["w_mlp2"].reshape(L, 32, 128, 8, 128).transpose(0, 3, 2, 1, 4), f32)
    sh.update(host_consts())
    cores = []
    B = inp["x"].shape[0]
    for b in range(B):
        d = {}
        d["xT"] = np.ascontiguousarray(inp["x"][b].reshape(8, 512, 8, 128).transpose(0, 3, 2, 1), f32)
        d["ctxT"] = np.ascontiguousarray(inp["ctx"][b].reshape(CTX, 8, 128).transpose(2, 1, 0), f32)
        ccv = np.zeros((128, 8, 2), f32)
        ccv[:, :, 0] = inp["c"][b].reshape(8, 128).T
        ccv[:, :, 1] = inp["c_ctx"].reshape(8, 128).T
        d["cc"] = ccv
        cores.append(d)
    return sh, cores


def build(nlayers=DEPTH, debug=()):
    nc = bass.Bass("TRN2", target_bir_lowering=False)
    es = ExitStack()
    with es:
        S = Sched(nc, es)

        def din(name, shape, dt=F32):
            return nc.dram_tensor(name, list(shape), dt, kind="ExternalInput").ap()

        def dscr(name, shape, dt):
            return nc.dram_tensor(name, list(shape), dt, kind="Internal").ap()

        xT = din("xT", [8, 128, 8, 512])
        ctxT = din("ctxT", [128, 8, CTX])
        cc_d = din("cc", [128, 8, 2])
        w_ada_d = din("w_ada", [DEPTH, 24, 128, 8, 256])
        eye2_d = din("eye2", [2, 2], F32)
        b_adaT_d = din("b_adaT", [DEPTH, 128, 48])
        g1T_d = din("g1T", [DEPTH, 128, 8])
        g2T_d = din("g2T", [DEPTH, 128, 8])
        w_in_d = din("w_in", [DEPTH, D, 1536])
        gqk_d = din("gqk", [DEPTH, 128, 2])
        vgb_d = din("vgb", [DEPTH, 128, 256])
        w_sT_d = din("w_sT", [DEPTH, 128, 4, 128])
        bsb_d = din("bsb", [DEPTH, 128, 2, 128])
        w_out_d = din("w_out", [DEPTH, D, D])
        w1r_d = din("w1r", [DEPTH, 8, 128, 8, 512])
        w2r_d = din("w2r", [DEPTH, 8, 128, 32, 128])
        cst = {}
        for nm in ("ones", "blk64", "perm", "bdc_lat", "bds_lat", "bdc_ctx", "bds_ctx"):
            cst[nm] = din(nm, [128, 128], BF16)
        cosT_d = din("cosT", [128, S_LAT], BF16)
        sinT_d = din("sinT", [128, S_LAT], BF16)
        dftc_d = din("dftc", [8, 128, 32, 512], BF16)
        dfts_d = din("dfts", [8, 128, 32, 512], BF16)
        dftc256_d = din("dftc256", [CTX, CTX], BF16)
        dfts256_d = din("dfts256", [CTX, CTX], BF16)

        outT = nc.dram_tensor("outT", [8, 128, 8, 512], F32, kind="ExternalOutput").ap()
        dbg_out = {}

        xa = dscr("xa", [8, 128, 8, 512], F32)
        xb = dscr("xb", [8, 128, 8, 512], F32)
        ca = dscr("ca", [128, 8, CTX], F32)
        cb = dscr("cb", [128, 8, CTX], F32)
        qT_lat = dscr("qT_lat", [8, 128, 4, 512], BF16)
        qT_ctx = dscr("qT_ctx", [128, 4, CTX], BF16)

        def qsl_d(g, lo=0, hi=128):
            return qT_ctx[lo:hi, :, :] if g == 0 else qT_lat[g - 1, lo:hi, :, :]
        w_in_bf = dscr("w_in_bf", [DEPTH, D, 1536], BF16)
        w_out_bf = dscr("w_out_bf", [DEPTH, D, D], BF16)
        w1_bf = dscr("w1_bf", [DEPTH, 8, 128, 8 * 512], BF16)
        w2_bf = dscr("w2_bf", [DEPTH, 8, 128, 32 * 128], BF16)

        def sb(name, shape, dt):
            return es.enter_context(nc.sbuf_tensor(name, list(shape), dt))

        UN = 34816
        U = sb("U", [128, UN], BF16)
        KT = U[:, 0:NTOK]
        VA = U[:, NTOK:NTOK + NKT * 256].rearrange("p (t c) -> p t c", c=256)
        o2 = NTOK + NKT * 256
        FT = U[:, o2:o2 + NKT * 256].rearrange("p (t c) -> p t c", c=256)
        o3 = o2 + NKT * 256
        COS = U[:, o3:o3 + S_LAT]
        SIN = U[:, o3 + S_LAT:o3 + 2 * S_LAT]
        assert o3 + 2 * S_LAT <= UN
        HID = U[:, 0:16384].rearrange("p (m t) -> p m t", t=512)
        W1B = [U[:, 16384 + i * 4096:16384 + (i + 1) * 4096].rearrange("p (k n) -> p k n", n=512) for i in range(2)]
        W2B = [U[:, 24576 + i * 4096:24576 + (i + 1) * 4096].rearrange("p (m c) -> p m c", c=128) for i in range(2)]
        WAB = [U[:, i * 6144:(i + 1) * 6144].rearrange("p (k f) -> p k f", f=768) for i in range(2)]

        WIN = sb("WIN", [128, 8, 1536], BF16)
        WOUT = sb("WOUT", [128, 8, 1024], BF16)
        XG = [sb(f"XG{i}", [128, 8, 512], F32) for i in range(2)]
        HT = [sb(f"HT{i}", [128, 8, 512], BF16) for i in range(2)]
        DFB = [[HT[i][:, 4 * ci:4 * ci + 4, :] for i in range(2)] for ci in range(2)]
        NDF = 5
        DF256 = None
        CST = {nm: sb("c_" + nm, [128, 128], BF16) for nm in cst}
        EYE2 = sb("EYE2", [2, 2], F32)
        P_bada = sb("P_bada", [128, DEPTH, 48], F32)
        P_g1 = sb("P_g1", [128, DEPTH, 8], F32)
        P_g2 = sb("P_g2", [128, DEPTH, 8], F32)
        P_gqk = sb("P_gqk", [128, DEPTH, 2], F32)
        P_vgb = sb("P_vgb", [128, DEPTH, 256], F32)
        P_wsT = sb("P_wsT", [128, DEPTH, 4, 128], BF16)
        P_bsb = sb("P_bsb", [128, DEPTH, 2, 128], F32)
        P_cc = sb("P_cc", [128, 8, 2], F32)
        SC = sb("SC", [128, 8, 2], BF16)
        MOD = sb("MOD", [128, 48, 2], F32)
        VEC = sb("VEC", [128, DEPTH, 6, 8, 2], F32)
        TAf = sb("TAf", [128, 5120], F32)
        T_f = [TAf[:, i * 512:(i + 1) * 512] for i in range(6)]
        RSTD = TAf[:, 3072:3584]
        GV = TAf[:, 3584:4096]
        GV2 = TAf[:, 4096:4608]
        RC = TAf[:, 4608:5120]
        SSV = sb("SSV", [128, 8], F32)
        RV = sb("RV", [128, 8], F32)
        TAb = sb("TAb", [128, 9728], BF16)
        T_b = [TAb[:, i * 512:(i + 1) * 512] for i in range(6)]
        QST = [TAb[:, 3072 + i * 2048:3072 + (i + 1) * 2048].rearrange("p (j t) -> p j t", t=512) for i in range(2)]
        GUT = TAb[:, 7168:8192].rearrange("p (c t) -> p c t", t=512)
        GMT = TAb[:, 8192:9216].rearrange("p (c t) -> p c t", t=512)
        GMTB = sb("GMTB", [128, 2, 512], BF16)
        DF256 = [TAb[:, 5120 + i * 512:5120 + (i + 1) * 512].rearrange("p (n k) -> p n k", k=256) for i in range(2)]
        GMT2 = [GMT, GMTB]
        VHN = TAb[:, 9216:9728]
        AB = [TAb[:, i * 512:(i + 1) * 512] for i in range(4)]
        YT = TAb[:, 3072:4096].rearrange("p (c t) -> p c t", t=512)
        WINf = WIN[:].rearrange("p k n -> p (k n)")
        QM = [[WINf[:, g * 2048:(g + 1) * 2048].rearrange("p (j t) -> p j t", t=512) for g in range(2)]]
        PT = [WINf[:, 4096 + i * 1024:4096 + (i + 1) * 1024] for i in range(3)]
        ATT = WINf[:, 7168:9216].rearrange("p (j t) -> p j t", t=512)
        assert 4096 + 3 * 1024 <= 7168
        PTH = [PT[i // 2][:, (i % 2) * 512:(i % 2 + 1) * 512] for i in range(6)]
        cHID = U[:, 13056:29440].rearrange("p (m t) -> p m t", t=512)
        cW2B = [U[:, 29440:33536].rearrange("p (m c) -> p m c", c=128),
                WINf[:, 8192:12288].rearrange("p (m c) -> p m c", c=128)]
        cW1B = [WINf[:, i * 4096:(i + 1) * 4096].rearrange("p (k n) -> p k n", n=512) for i in range(2)]
        cQM = [TAb[:, 3072 + g * 2048:3072 + (g + 1) * 2048].rearrange("p (j t) -> p j t", t=512) for g in range(2)]
        cPT = [TAb[:, 7168 + i * 512:7168 + (i + 1) * 512] for i in range(4)]
        WOUTf = WOUT[:].rearrange("p k n -> p (k n)")
        cATT = WOUTf[:, 4096:6144].rearrange("p (j t) -> p j t", t=512)
        for ci in range(2):
            for i in range(3):
                off = (i * 2 + ci) * 2048
                DFB[ci].append(WINf[:, off:off + 2048].rearrange("p (n k) -> p n k", k=512))

        PSW = [es.enter_context(nc.psum_tensor(f"psw{i}", [128, 1024], F32)) for i in range(4)]
        PS = [PSW[i // 2][:, (i % 2) * 512:(i % 2 + 1) * 512] for i in range(8)]

        def B(n):
            return Buf(n)

        bU = B("U")
        bKT = [B(f"KT{g}") for g in range(9)]
        bVA = [B(f"VA{g}") for g in range(9)]
        bFT = [B(f"FT{g}") for g in range(9)]
        bTAB = B("TAB")
        bHID = B("HID")
        bW1B = [B(f"W1B{i}") for i in range(2)]
        bW2B = [B(f"W2B{i}") for i in range(2)]
        bWAB = [B(f"WAB{i}") for i in range(2)]
        bWIN, bWOUT = B("WIN"), B("WOUT")
        bXG = [B(f"XG{i}") for i in range(2)]
        bHT = [B(f"HT{i}") for i in range(2)]
        bDFB = [[B(f"DF{cs}{i}") for i in range(5)] for cs in "cs"]
        bDF256 = B("DF256")
        bCST = B("CST")
        bPAR = B("PAR")
        bPARW = B("PARW")
        bSC, bMOD, bVEC = B("SC"), B("MOD"), B("VEC")
        bTf = [B(f"Tf{i}") for i in range(6)]
        bTb = [B(f"Tb{i}") for i in range(6)]
        bRSTD = B("RSTD")
        bQST = [B(f"QST{i}") for i in range(2)]
        bQM = [B("QM0")]
        bGUT, bGMT, bGV, bGV2, bVHN, bSSV, bRV = B("GUT"), B("GMT"), B("GV"), B("GV2"), B("VHN"), B("SSV"), B("RV")
        bGMT2 = [bGMT, B("GMTB")]
        bPT = [B(f"PT{i}") for i in range(3)]
        bPTH = [B(f"PTH{i}") for i in range(6)]
        bcHID, bcQM, bcATT = B("cHID"), B("cQM"), B("cATT")
        bcW1B = [B(f"cW1B{i}") for i in range(2)]
        bcW2B = [B(f"cW2B{i}") for i in range(2)]
        bcPT = [B(f"cPT{i}") for i in range(4)]
        bATT, bRC = B("ATT"), B("RC")
        bAB = [B(f"AB{i}") for i in range(4)]
        bYT = B("YT")
        bPS = [B(f"PS{i}") for i in range(8)]
        for b_ in bPS:
            b_.excl = True
        bXD = {}
        for nm in ("xT", "xa", "xb", "outT", "ctxT", "ca", "cb"):
            for g in range(9):
                bXD[(nm, g)] = B(f"{nm}{g}")
        bQD = [B(f"QD{g}") for g in range(9)]
        bWC = {}
        for nm in ("w_in", "w_out", "w1", "w2"):
            for l in range(DEPTH):
                bWC[(nm, l)] = B(f"WC{nm}{l}")
                bWC[(nm, l)].nobar = True

        def mm(out, lhsT, rhs, start, stop, r, w, sig=False):
            S.op("pe", lambda e: e.matmul(out, lhsT=lhsT, rhs=rhs, start=start, stop=stop), r, w, sig)

        def act(out, in_, func, r, w, bias=None, scale=None):
            kw = {}
            if bias is not None:
                kw["bias"] = bias
            if scale is not None:
                kw["scale"] = scale
            S.op("act", lambda e: e.activation(out=out, in_=in_, func=func, **kw), r, w)

        def tt(en, out, in0, in1, op, r, w):
            S.op(en, lambda e: e.tensor_tensor(out=out, in0=in0, in1=in1, op=op), r, w)

        def ts(en, out, in0, s1, s2, op0, op1, r, w):
            if op1 is None:
                S.op(en, lambda e: e.tensor_scalar(out=out, in0=in0, scalar1=s1, scalar2=None, op0=op0), r, w)
            else:
                S.op(en, lambda e: e.tensor_scalar(out=out, in0=in0, scalar1=s1, scalar2=s2, op0=op0, op1=op1), r, w)

        def stt(out, in0, scalar, in1, op0, op1, r, w):
            S.op("dve", lambda e: e.scalar_tensor_tensor(out=out, in0=in0, scalar=scalar, in1=in1, op0=op0, op1=op1), r, w)

        def cp(en, out, in_, r, w):
            if en == "act":
                S.op(en, lambda e: e.activation(out=out, in_=in_, func=AF.Identity), r, w)
            else:
                S.op(en, lambda e: e.tensor_copy(out=out, in_=in_), r, w)

        def ckpt(name):
            if ("ck:" + name) in debug:
                raise _Stop()

        def dbg_dump(name, ap, shape, dt, bufs):
            if name not in debug:
                return
            d = nc.dram_tensor("dbg_" + name, list(shape), dt, kind="ExternalOutput").ap()
            dbg_out[name] = d
            S.dma("sp", d, ap, bufs, [], bufs[0])

        for nm in cst:
            S.dma("sp", CST[nm][:], cst[nm][:, :], [], [bCST], bCST)
        S.dma("sp", EYE2[:], eye2_d[:, :], [], [bCST], bCST)
        S.dma("sp", P_cc[:], cc_d[:, :, :], [], [bPAR], bPAR)
        for l in range(DEPTH):
            S.dma("sp", P_bada[:, l, :], b_adaT_d[l], [], [bPAR], bPAR)
            S.dma("sp", P_g1[:, l, :], g1T_d[l], [], [bPAR], bPAR)
            S.dma("sp", P_g2[:, l, :], g2T_d[l], [], [bPAR], bPAR)
            S.dma("sp", P_gqk[:, l, :], gqk_d[l], [], [bPAR], bPAR)
            S.dma("sp", P_vgb[:, l, :], vgb_d[l], [], [bPAR], bPAR)
            S.dma("sp", P_bsb[:, l, :, :], bsb_d[l], [], [bPAR], bPAR)
            S.dma("pool", P_wsT[:, l, :, :], w_sT_d[l], [], [bPARW], bPARW)

        act(SC[:], P_cc[:], AF.Silu, [bPAR], [bSC])

        WAB3 = [U[:, i * 4096:(i + 1) * 4096].rearrange("p (k f) -> p k f", f=512) for i in range(3)]
        bWAB3 = [B(f"WAB3_{i}") for i in range(3)]
        wslot = 0
        STG = [XG[i // 2][:, :, (i % 2) * 256:(i % 2 + 1) * 256] for i in range(4)]
        bSTG = [B(f"STG{i}") for i in range(4)]
        bWABp = [[B(f"WABp{i}_{p_}") for p_ in range(3)] for i in range(3)]
        parts = [("dve", 0, 4), ("act", 4, 7), ("pool", 7, 8)]
        for l in range(nlayers):
            pend = None

            def transposes(p):
                pj, ti = p
                for fc in range(2):
                    ch = pj * 2 + fc
                    mm(PS[0][:, ch * 2:ch * 2 + 2], T_f[ti][0:2, fc * 128:(fc + 1) * 128], EYE2[:, :], True, True,
                       [bTf[ti], bCST], [bPS[0]], sig=(fc == 1))

            for j in range(24):
                sl = wslot % 4
                wl = wslot % 3
                wslot += 1
                S.dma("sp", STG[sl], w_ada_d[l, j], [], [bSTG[sl]], bSTG[sl])
                wv = WAB3[wl][:, :, 0:256]
                for pi_, (en, lo, hi) in enumerate(parts):
                    cp(en, wv[:, lo:hi, :], STG[sl][:, lo:hi, :], [bSTG[sl]], [bWABp[wl][pi_]])
                pi = 1 + j % 2
                for dk in range(8):
                    pb = bWABp[wl][0 if dk < 4 else (1 if dk < 7 else 2)]
                    mm(PS[pi][0:2, 0:256], SC[:, dk, :], wv[:, dk, :], dk == 0, dk == 7, [pb, bSC], [bPS[pi]], sig=True)
                ti = j % 2
                cp("dve", T_f[ti][0:2, 0:256], PS[pi][0:2, 0:256], [bPS[pi]], [bTf[ti]])
                if pend is not None:
                    transposes(pend)
                pend = (j, ti)
            transposes(pend)
            tt("dve", MOD[:], PS[0][:, 0:96].rearrange("p (c v) -> p c v", v=2),
               P_bada[:, l, :].unsqueeze(2).broadcast_to([128, 48, 2]), ALU.add, [bPS[0], bPAR], [bMOD])
            stt(VEC[:, l, 0, :, :], MOD[:, 8:16, :], 1.0, P_g1[:, l, :].unsqueeze(2).broadcast_to([128, 8, 2]),
                ALU.add, ALU.mult, [bMOD, bPAR], [bVEC])
            cp("dve", VEC[:, l, 1, :, :], MOD[:, 0:8, :], [bMOD], [bVEC])
            cp("dve", VEC[:, l, 2, :, :], MOD[:, 16:24, :], [bMOD], [bVEC])
            stt(VEC[:, l, 3, :, :], MOD[:, 32:40, :], 1.0, P_g2[:, l, :].unsqueeze(2).broadcast_to([128, 8, 2]),
                ALU.add, ALU.mult, [bMOD, bPAR], [bVEC])
            cp("dve", VEC[:, l, 4, :, :], MOD[:, 24:32, :], [bMOD], [bVEC])
            cp("dve", VEC[:, l, 5, :, :], MOD[:, 40:48, :], [bMOD], [bVEC])
        dbg_dump("vec", VEC[:], [128, DEPTH, 6, 8, 2], F32, [bVEC])

        def cast_w(l, which):
            if which == "b":
                S.dma("pool", w1_bf[l].rearrange("j p (a n) -> (j p a) n", n=2048),
                      w1r_d[l].rearrange("j p k n -> (j p) (k n)").rearrange("r (a n) -> (r a) n", n=2048),
                      [], [bWC[("w1", l)]], bWC[("w1", l)])
                S.dma("pool", w2_bf[l].rearrange("o p (a n) -> (o p a) n", n=2048),
                      w2r_d[l].rearrange("o p m c -> (o p) (m c)").rearrange("r (a n) -> (r a) n", n=2048),
                      [], [bWC[("w2", l)]], bWC[("w2", l)])
                return
            S.dma("pool", w_in_bf[l].rearrange("k (a n) -> (k a) n", n=768),
                  w_in_d[l].rearrange("k (a n) -> (k a) n", n=768), [], [bWC[("w_in", l)]], bWC[("w_in", l)])
            S.dma("pool", w_out_bf[l], w_out_d[l], [], [bWC[("w_out", l)]], bWC[("w_out", l)])

        cast_queue = []

        def run_casts(n=100):
            while cast_queue and n > 0:
                l_, w_ = cast_queue.pop(0)
                cast_w(l_, w_)
                n -= 1

        groups = [(0, CTX, 0, True)] + [(i * 512, 512, CTX + i * 512, False) for i in range(8)]

        def xsrc(ap, t0, NT):
            if NT == CTX:
                return ap[:, :, :]
            return ap[t0 // 512]

        state = {"xslot": 0, "hslot": 0, "psr": 0, "dfslot": 0}

        def interleave(gens):
            gens = list(gens)
            while gens:
                for gen in list(gens):
                    try:
                        next(gen)
                    except StopIteration:
                        gens.remove(gen)

        def norm_gen(l, xg, bxg, NT, which, v, out, ssb=0):
            hs = state["hslot"] % 2
            state["hslot"] += 1
            for kc in range(8):
                si = 2 + kc % 4
                en = ("pool", "act", "dve", "act")[kc % 4]
                if en == "act":
                    act(T_b[si][:, 0:NT], xg[:, kc, 0:NT], AF.Square, [bxg], [bTb[si]])
                else:
                    tt(en, T_b[si][:, 0:NT], xg[:, kc, 0:NT], xg[:, kc, 0:NT], ALU.mult, [bxg], [bTb[si]])
                mm(PS[ssb][:, 0:NT], CST["ones"][:], T_b[si][:, 0:NT], kc == 0, kc == 7, [bCST, bTb[si]], [bPS[ssb]], sig=True)
                if kc % 4 == 3:
                    yield
            act(RSTD[:, 0:NT], PS[ssb][:, 0:NT], AF.Ln, [bPS[ssb]], [bRSTD], bias=EPS, scale=1.0 / D)
            act(RSTD[:, 0:NT], RSTD[:, 0:NT], AF.Exp, [bRSTD], [bRSTD], scale=-0.5)
            yield
            for kc in range(8):
                ti = kc % 2
                tt("dve", T_f[ti][:, 0:NT], xg[:, kc, 0:NT], RSTD[:, 0:NT], ALU.mult, [bxg, bRSTD], [bTf[ti]])
                if kc % 3 == 2:
                    ts("pool", HT[hs][:, kc, 0:NT], T_f[ti][:, 0:NT], VEC[:, l, which, kc, v:v + 1],
                       VEC[:, l, which + 1, kc, v:v + 1], ALU.mult, ALU.add, [bTf[ti], bVEC], [bHT[hs]])
                else:
                    act(HT[hs][:, kc, 0:NT], T_f[ti][:, 0:NT], AF.Identity, [bTf[ti], bVEC], [bHT[hs]],
                        bias=VEC[:, l, which + 1, kc, v:v + 1], scale=VEC[:, l, which, kc, v:v + 1])
                yield
            out.append((HT[hs], bHT[hs]))

        def norm_mod(l, g, xg, bxg, NT, which, v):
            out = []
            for _ in norm_gen(l, xg, bxg, NT, which, v, out):
                pass
            return out[0]

        def load_x(src, srcname, g, t0, NT):
            xs = state["xslot"] % 2
            state["xslot"] += 1
            S.dma("sp", XG[xs][:, :, 0:NT], xsrc(src, t0, NT), [bXD[(srcname, g)]], [bXG[xs]], bXG[xs])
            return XG[xs], bXG[xs]

        def store_x(dst, dstname, g, t0, NT, xg, bxg):
            S.dma("sp", xsrc(dst, t0, NT), xg[:, :, 0:NT], [bxg], [bXD[(dstname, g)]], bxg)

        def residual(l, which, v, o, ps, bps, xg, bxg, NT):
            stt(xg[:, o, 0:NT], ps[:, 0:NT], VEC[:, l, which, o, v:v + 1], xg[:, o, 0:NT], ALU.mult, ALU.add,
                [bps, bVEC, bxg], [bxg])

        def phase_A(l, srcs, dsts):
            last = (l == DEPTH - 1)
            S.dma("sp", WIN[:], w_in_bf[l].rearrange("(c p) n -> p c n", p=128), [bWC[("w_in", l)]], [bWIN], bWIN)
            S.dma("sp", WOUT[:], w_out_bf[l].rearrange("(c p) n -> p c n", p=128), [bWC[("w_out", l)]], [bWOUT], bWOUT)
            S.dma("sp", COS, cosT_d[:, :], [], [bTAB], bTAB)
            S.dma("sp", SIN, sinT_d[:, :], [], [bTAB], bTAB)
            S.op("pool", lambda e: e.memset(VA[:, :, 64:192], 1.0), [], bVA)

            def fm_mm(ci, ht, bht, NT):
                pi = 1 + (state["psr"] % 2)
                state["psr"] += 1
                ps, bps = PS[pi], bPS[pi]
                for kc in range(8):
                    mm(ps[:, 0:NT], WIN[:, kc, ci * 128:(ci + 1) * 128], ht[:, kc, 0:NT], kc == 0, kc == 7,
                       [bWIN, bht], [bps], sig=(kc == 7))
                return ps, bps

            def fm_epi(g, kind, ci, ps, bps, NT, t0, kp0, is_ctx, qs):
                if kind == "u":
                    act(GUT[:, ci - 5, 0:NT], ps[:, 0:NT], AF.Gelu_apprx_tanh, [bps], [bGUT])
                    return
                gcol = 0 if kind == "q" else 1
                if kind == "q":
                    dest, bdest = QST[qs][:, ci, 0:NT], bQST[qs]
                else:
                    dest, bdest = KT[:, kp0:kp0 + NT], bKT[g]
                act(T_b[0][:, 0:NT], ps[:, 0:NT], AF.Square, [bps], [bTb[0]])
                ts("dve", T_b[1][:, 0:NT], ps[:, 0:NT], P_gqk[:, l, gcol:gcol + 1], None, ALU.mult, None, [bps, bPAR], [bTb[1]])
                yield
                mm(PS[3][:, 0:NT], CST["blk64"][:], T_b[0][:, 0:NT], True, True, [bCST, bTb[0]], [bPS[3]], sig=True)
                if not is_ctx:
                    mm(PS[4][:, 0:NT], CST["perm"][:], T_b[1][:, 0:NT], True, True, [bCST, bTb[1]], [bPS[4]], sig=True)
                yield
                act(T_f[2][:, 0:NT], PS[3][:, 0:NT], AF.Ln, [bPS[3]], [bTf[2]], bias=EPS, scale=1.0 / 64)
                if not is_ctx:
                    tt("dve", T_f[3][:, 0:NT], T_b[1][:, 0:NT], COS[:, t0:t0 + NT], ALU.mult, [bTb[1], bTAB], [bTf[3]])
                yield
                act(T_f[2][:, 0:NT], T_f[2][:, 0:NT], AF.Exp, [bTf[2]], [bTf[2]], scale=-0.5)
                if is_ctx:
                    yield
                    tt("dve", dest, T_b[1][:, 0:NT], T_f[2][:, 0:NT], ALU.mult, [bTb[1], bTf[2]], [bdest])
                    return
                tt("dve", T_f[4][:, 0:NT], PS[4][:, 0:NT], SIN[:, t0:t0 + NT], ALU.mult, [bPS[4], bTAB], [bTf[4]])
                yield
                tt("pool", T_f[3][:, 0:NT], T_f[3][:, 0:NT], T_f[4][:, 0:NT], ALU.add, [bTf[3], bTf[4]], [bTf[3]])
                yield
                tt("dve", dest, T_f[3][:, 0:NT], T_f[2][:, 0:NT], ALU.mult, [bTf[3], bTf[2]], [bdest])

            def fm_chain(g, chunks, ht, bht, NT, t0, kp0, is_ctx, qs, full):
                nxt = fm_mm(chunks[0][1], ht, bht, NT)
                for i, (kind, ci) in enumerate(chunks):
                    ps, bps = nxt
                    if i + 1 < len(chunks):
                        nxt = fm_mm(chunks[i + 1][1], ht, bht, NT)
                    yield
                    for _ in fm_epi(g, kind, ci, ps, bps, NT, t0, kp0, is_ctx, qs):
                        yield
                    yield
                if full:
                    S.dma("sp", qsl_d(g), QST[qs][:, :, 0:NT], [bQST[qs]], [bQD[g]], bQST[qs])

            def tok_chain(g, ht, bht, NT, kp0, full):
                ntile = NT // 128
                ncol = 384 if full else 128
                for tl in range(ntile):
                    kt = kp0 // 128 + tl
                    tsl = slice(tl * 128, (tl + 1) * 128)
                    for kc in range(8):
                        mm(PS[5][:, 0:ncol], ht[:, kc, tsl], WIN[:, kc, 896:896 + ncol], kc == 0, kc == 7,
                           [bWIN, bht], [bPS[5]], sig=(kc == 7))
                    yield
                    cp("dve", VA[:, kt, 0:64], PS[5][:, 0:64], [bPS[5]], [bVA[g]])
                    cp("dve", VA[:, kt, 192:256], PS[5][:, 64:128], [bPS[5]], [bVA[g]])
                    if full:
                        cp("act", FT[:, kt, :], PS[5][:, 128:384], [bPS[5]], [bFT[g]])
                    yield
                if not full:
                    return
                for bt in range(ntile // 2):
                    for t in range(2):
                        tsl = slice((bt * 2 + t) * 128, (bt * 2 + t + 1) * 128)
                        for kc in range(8):
                            mm(PS[6][:, t * 256:(t + 1) * 256], ht[:, kc, tsl], WIN[:, kc, 1280:1536], kc == 0, kc == 7,
                               [bWIN, bht], [bPS[6]], sig=(kc == 7))
                    yield
                    act(GV[:], PS[6][:, :], AF.Gelu_apprx_tanh, [bPS[6]], [bGV])
                    yield
                    tt("pool", GV2[:], GV[:], GV[:], ALU.mult, [bGV], [bGV2])
                    yield
                    S.op("dve", lambda e: e.tensor_reduce(out=SSV[:], in_=GV2[:].rearrange("p (h d) -> p h d", d=64),
                                                         axis=AX.X, op=ALU.add), [bGV2], [bSSV])
                    yield
                    act(RV[:], SSV[:], AF.Ln, [bSSV], [bRV], bias=EPS, scale=1.0 / 64)
                    act(RV[:], RV[:], AF.Exp, [bRV], [bRV], scale=-0.5)
                    yield
                    tt("dve", GV2[:].rearrange("p (h d) -> p h d", d=64), GV[:].rearrange("p (h d) -> p h d", d=64),
                       RV[:, 0:8].unsqueeze(2).broadcast_to([128, 8, 64]), ALU.mult, [bGV, bRV], [bGV2])
                    yield
                    tt("pool", VHN[:].rearrange("p (t c) -> p t c", c=256), GV2[:].rearrange("p (t c) -> p t c", c=256),
                       P_vgb[:, l, :].unsqueeze(1).broadcast_to([128, 2, 256]), ALU.mult, [bGV2, bPAR], [bVHN])
                    yield
                    ps7 = PS[6][:, :].rearrange("p (c t q) -> p c t q", c=2, t=2)
                    for t in range(2):
                        for c in range(2):
                            for hl in range(2):
                                h = 2 * c + hl
                                mm(ps7[hl * 64:(hl + 1) * 64, c, t, :], VHN[:, t * 256 + h * 64:t * 256 + (h + 1) * 64],
                                   P_wsT[:, l, h, :], True, True, [bVHN, bPARW], [bPS[6]], sig=(t == 1 and c == 1 and hl == 1))
                    yield
                    t5 = T_f[5][:, :].rearrange("p (c t q) -> p c t q", c=2, t=2)
                    tt("dve", t5, ps7, P_bsb[:, l, :, :].unsqueeze(2).broadcast_to([128, 2, 2, 128]), ALU.add,
                       [bPS[6], bPAR], [bTf[5]])
                    yield
                    bsl = slice(bt * 256, (bt + 1) * 256)
                    tt("dve", GMT2[g % 2][:, :, bsl], T_f[5][:, :].rearrange("p (c n) -> p c n", c=2), GUT[:, :, bsl], ALU.mult,
                       [bTf[5], bGUT], [bGMT2[g % 2]])
                    yield

            def prep_gen(gi, out, delay=0):
                t0_, NT_, kp0_, is_ctx_ = groups[gi]
                src_, srcname_ = srcs[1] if is_ctx_ else srcs[0]
                for _ in range(delay):
                    yield
                xg_, bxg_ = load_x(src_, srcname_, gi, t0_, NT_)
                res = []
                yield
                yield
                for _ in norm_gen(l, xg_, bxg_, NT_, 0, 1 if is_ctx_ else 0, res):
                    yield
                out.append((xg_, bxg_) + res[0])

            def tail_gen(g, v, NT, t0, xg, bxg, dst, dstname):
                ps, bps = PS[7], bPS[7]
                for o in range(8):
                    for c in range(2):
                        mm(ps[:, 0:NT], WOUT[:, 6 + c, o * 128:(o + 1) * 128], GMT2[g % 2][:, c, 0:NT], c == 0, c == 1,
                           [bWOUT, bGMT2[g % 2]], [bps], sig=(c == 1))
                    residual(l, 2, v, o, ps, bps, xg, bxg, NT)
                    yield
                store_x(dst, dstname, g, t0, NT, xg, bxg)

            tail_pending = None
            cur = []
            for _ in prep_gen(0, cur):
                pass
            for g, (t0, NT, kp0, is_ctx) in enumerate(groups):
                v = 1 if is_ctx else 0
                dst, dstname = dsts[1] if is_ctx else dsts[0]
                full = not (is_ctx and last)
                xg, bxg, ht, bht = cur[0]
                nxt = []
                qs = g % 2
                chunks = []
                if full:
                    chunks += [("u", 5), ("u", 6)]
                    chunks += [("q", j) for j in range(4)]
                chunks.append(("k", 4))
                chains = [fm_chain(g, chunks, ht, bht, NT, t0, kp0, is_ctx, qs, full),
                          tok_chain(g, ht, bht, NT, kp0, full)]
                delay = 0
                if tail_pending is not None:
                    chains.insert(0, tail_pending)
                    tail_pending = None
                    delay = 9
                if g + 1 < len(groups):
                    chains.append(prep_gen(g + 1, nxt, delay))
                interleave(chains)
                cur = nxt
                if g == 1:
                    run_casts(1)
                if not full:
                    continue
                tail_pending = tail_gen(g, v, NT, t0, xg, bxg, dst, dstname)
            if tail_pending is not None:
                for _ in tail_pending:
                    pass

        def phase_B(l, srcs, dsts):
            last = (l == DEPTH - 1)
            glist = [(g, grp) for g, grp in enumerate(groups) if not (grp[3] and last)]
            run_casts()
            if not last:
                for dd, tt_ in ((dftc256_d, DF256[0]), (dfts256_d, DF256[1])):
                    S.dma("sp", tt_, dd.rearrange("(n p) k -> p n k", p=128), [], [bDF256], bDF256)

            def dft_gen(idx, out):
                g, (t0, NT, kp0, is_ctx) = glist[idx]
                src, srcname = srcs[1] if is_ctx else srcs[0]
                xg, bxg = load_x(src, srcname, g, t0, NT)
                if is_ctx:
                    nts, kt0 = 2, 0
                else:
                    nts, kt0 = 32, 2
                rF = [bFT[0]] if is_ctx else bFT[1:]
                for nt in range(nts):
                    if is_ctx:
                        dc, ds, bdc, bds, ii = DF256[0], DF256[1], bDF256, bDF256, nt
                    else:
                        ii = nt % 4
                        sl = state["dfslot"] % NDF
                        if ii == 3:
                            state["dfslot"] += 1
                        dc, ds, bdc, bds = DFB[0][sl], DFB[1][sl], bDFB[0][sl], bDFB[1][sl]
                        if ii == 0:
                            n4 = nt // 4
                            S.dma("sp", dc[:], dftc_d[t0 // 512, :, n4 * 4:(n4 + 1) * 4, :], [], [bdc], bdc)
                            S.dma("sp", ds[:], dfts_d[t0 // 512, :, n4 * 4:(n4 + 1) * 4, :], [], [bds], bds)
                    for c in range(2):
                        mm(PS[c][:, 0:NT], FT[:, kt0 + nt, c * 128:(c + 1) * 128], dc[:, ii, 0:NT], nt == 0, nt == nts - 1,
                           rF + [bdc], [bPS[c]], sig=(nt == nts - 1 or ii == 3))
                        mm(PS[2 + c][:, 0:NT], FT[:, kt0 + nt, c * 128:(c + 1) * 128], ds[:, ii, 0:NT], nt == 0, nt == nts - 1,
                           rF + [bds], [bPS[2 + c]], sig=(nt == nts - 1 or ii == 3))
                    yield
                out.append((xg, bxg))

            def tail_gen(idx, xg, bxg):
                g, (t0, NT, kp0, is_ctx) = glist[idx]
                v = 1 if is_ctx else 0
                dst, dstname = dsts[1] if is_ctx else dsts[0]
                for c in range(2):
                    cp("dve", AB[c][:, 0:NT], PS[c][:, 0:NT], [bPS[c]], [bAB[c]])
                    cp("act", AB[2 + c][:, 0:NT], PS[2 + c][:, 0:NT], [bPS[2 + c]], [bAB[2 + c]])
                yield
                cname, sname = ("bdc_ctx", "bds_ctx") if is_ctx else ("bdc_lat", "bds_lat")
                for c in range(2):
                    mm(PS[4 + c][:, 0:NT], CST[cname][:], AB[c][:, 0:NT], True, False, [bCST, bAB[c]], [bPS[4 + c]])
                    mm(PS[4 + c][:, 0:NT], CST[sname][:], AB[2 + c][:, 0:NT], False, True, [bCST, bAB[2 + c]], [bPS[4 + c]], sig=True)
                yield
                for c in range(2):
                    cp("dve" if c == 0 else "act", YT[:, c, 0:NT], PS[4 + c][:, 0:NT], [bPS[4 + c]], [bYT])
                yield
                for o in range(8):
                    pi = 6 + (o % 2)
                    ps, bps = PS[pi], bPS[pi]
                    for c in range(2):
                        mm(ps[:, 0:NT], WOUT[:, 4 + c, o * 128:(o + 1) * 128], YT[:, c, 0:NT], c == 0, c == 1,
                           [bWOUT, bYT], [bps], sig=(c == 1))
                    yield
                    residual(l, 2, v, o, ps, bps, xg, bxg, NT)
                    yield
                store_x(dst, dstname, g, t0, NT, xg, bxg)

            cur = []
            for _ in dft_gen(0, cur):
                pass
            for idx in range(len(glist)):
                xg, bxg = cur[0]
                nxt = []
                chains = [tail_gen(idx, xg, bxg)]
                if idx + 1 < len(glist):
                    chains.append(dft_gen(idx + 1, nxt))
                interleave(chains)
                cur = nxt

        def phase_C(l, srcs, dsts):
            last = (l == DEPTH - 1)
            for i in range(1):
                S.op("pool", lambda e, i=i: e.memset(QM[i][0][64:128, :, :], 0.0), [], [bQM[i]])
                S.op("pool", lambda e, i=i: e.memset(QM[i][1][0:64, :, :], 0.0), [], [bQM[i]])
            pslot = 0
            qtc = 0
            for g, (t0, NT, kp0, is_ctx) in enumerate(groups):
                if is_ctx and last:
                    continue
                v = 1 if is_ctx else 0
                src, srcname = srcs[1] if is_ctx else srcs[0]
                dst, dstname = dsts[1] if is_ctx else dsts[0]
                xg, bxg = load_x(src, srcname, g, t0, NT)
                qi = 0
                S.dma("sp", QM[qi][0][0:64, :, 0:NT], qsl_d(g, 0, 64), [bQD[g]], [bQM[qi]], bQM[qi])
                S.dma("sp", QM[qi][1][64:128, :, 0:NT], qsl_d(g, 64, 128), [bQD[g]], [bQM[qi]], bQM[qi])
                kts = [0, 1] if is_ctx else list(range(NKT))
                rK = [bKT[0]] if is_ctx else bKT
                rV = [bVA[0]] if is_ctx else bVA
                for qt in range(NT // 128):
                    qsl = slice(qt * 128, (qt + 1) * 128)
                    ob = 4 + 2 * (qtc % 2)
                    qtc += 1
                    pend = None
                    nk = len(kts)

                    def pv(p):
                        for (gk, ppt, pkt, pki) in p:
                            mm(PS[ob + gk][:, :], VA[:, pkt, gk * 128:(gk + 1) * 128], PTH[ppt][:, :],
                               pki == 0, pki == nk - 1, rV + [bPTH[ppt]], [bPS[ob + gk]], sig=True)

                    for ki, kt in enumerate(kts):
                        cur = []
                        for gk in range(2):
                            sbank = (pslot % 2) * 2 + gk
                            mm(PS[sbank][:, :], KT[:, kt * 128:(kt + 1) * 128], QM[qi][gk][:, :, qsl], True, True,
                               rK + [bQM[qi]], [bPS[sbank]], sig=True)
                            pt = (pslot % 3) * 2 + gk
                            act(PTH[pt][:, :], PS[sbank][:, :], AF.Exp, [bPS[sbank]], [bPTH[pt]], scale=0.125)
                            cur.append((gk, pt, kt, ki))
                        pslot += 1
                        if pend is not None:
                            pv(pend)
                        pend = cur
                    pv(pend)
                    S.op("dve", lambda e, ob=ob: e.reciprocal(out=RC[0:64, :], in_=PS[ob][64:128, :]), [bPS[ob]], [bRC])
                    S.op("dve", lambda e, ob=ob: e.reciprocal(out=RC[64:128, :], in_=PS[ob + 1][0:64, :]), [bPS[ob + 1]], [bRC])
                    tt("dve", ATT[0:64, :, qsl], PS[ob][0:64, :].rearrange("p (j q) -> p j q", j=4),
                       RC[0:64, :].rearrange("p (j q) -> p j q", j=4), ALU.mult, [bPS[ob], bRC], [bATT])
                    tt("dve", ATT[64:128, :, qsl], PS[ob + 1][64:128, :].rearrange("p (j q) -> p j q", j=4),
                       RC[64:128, :].rearrange("p (j q) -> p j q", j=4), ALU.mult, [bPS[ob + 1], bRC], [bATT])
                for o in range(8):
                    pi = o % 4
                    ps, bps = PS[pi], bPS[pi]
                    for j in range(4):
                        mm(ps[:, 0:NT], WOUT[:, j, o * 128:(o + 1) * 128], ATT[:, j, 0:NT], j == 0, j == 3,
                           [bWOUT, bATT], [bps], sig=(j == 3))
                    residual(l, 2, v, o, ps, bps, xg, bxg, NT)
                store_x(dst, dstname, g, t0, NT, xg, bxg)

        def phase_D(l, srcs, dsts):
            last = (l == DEPTH - 1)
            glist = [(g, grp) for g, grp in enumerate(groups) if not (grp[3] and last)]

            def prep(idx):
                g, (t0, NT, kp0, is_ctx) = glist[idx]
                src, srcname = srcs[1] if is_ctx else srcs[0]
                xg, bxg = load_x(src, srcname, g, t0, NT)
                ht, bht = norm_mod(l, g, xg, bxg, NT, 3, 1 if is_ctx else 0)
                return xg, bxg, ht, bht

            cur = prep(0)
            wj = 0
            for idx, (g, (t0, NT, kp0, is_ctx)) in enumerate(glist):
                v = 1 if is_ctx else 0
                dst, dstname = dsts[1] if is_ctx else dsts[0]
                xg, bxg, ht, bht = cur
                for j in range(8):
                    sl = wj % 2
                    wj += 1
                    S.dma("sp", W1B[sl], w1_bf[l, j].rearrange("p (k n) -> p k n", n=512), [bWC[("w1", l)]], [bW1B[sl]], bW1B[sl])
                    for mi in range(4):
                        m = j * 4 + mi
                        pi = 1 + (m % 4)
                        ps, bps = PS[pi], bPS[pi]
                        for kc in range(8):
                            mm(ps[:, 0:NT], W1B[sl][:, kc, mi * 128:(mi + 1) * 128], ht[:, kc, 0:NT], kc == 0, kc == 7,
                               [bW1B[sl], bht], [bps], sig=(kc == 7))
                        ti = 2 + m % 2
                        act(T_f[ti][:, 0:NT], ps[:, 0:NT], AF.Square, [bps], [bTf[ti]])
                        stt(HID[:, m, 0:NT], ps[:, 0:NT], 0.0, T_f[ti][:, 0:NT], ALU.is_gt, ALU.mult, [bps, bTf[ti]], [bHID])
                nxt = None
                for o in range(8):
                    if o == 4 and idx + 1 < len(glist):
                        nxt = prep(idx + 1)
                    sl = wj % 2
                    wj += 1
                    S.dma("sp", W2B[sl], w2_bf[l, o].rearrange("p (m c) -> p m c", c=128), [bWC[("w2", l)]], [bW2B[sl]], bW2B[sl])
                    pi = 5 + (o % 3)
                    ps, bps = PS[pi], bPS[pi]
                    for m in range(32):
                        mm(ps[:, 0:NT], W2B[sl][:, m, :], HID[:, m, 0:NT], m == 0, m == 31, [bW2B[sl], bHID], [bps], sig=(m == 31))
                    residual(l, 5, v, o, ps, bps, xg, bxg, NT)
                store_x(dst, dstname, g, t0, NT, xg, bxg)
                cur = nxt

        def phase_CD(l, srcs, dsts):
            last = (l == DEPTH - 1)
            glist = [(g, grp) for g, grp in enumerate(groups) if not (grp[3] and last)]
            S.op("pool", lambda e: e.memset(cQM[0][64:128, :, :], 0.0), [], [bcQM])
            S.op("pool", lambda e: e.memset(cQM[1][0:64, :, :], 0.0), [], [bcQM])
            st = {"pslot": 0, "wj1": 0, "wj2": 0}

            def attn_gen(idx, out):
                g, (t0, NT, kp0, is_ctx) = glist[idx]
                v = 1 if is_ctx else 0
                src, srcname = srcs[1] if is_ctx else srcs[0]
                xg, bxg = load_x(src, srcname, g, t0, NT)
                S.dma("sp", cQM[0][0:64, :, 0:NT], qsl_d(g, 0, 64), [bQD[g]], [bcQM], bcQM)
                S.dma("sp", cQM[1][64:128, :, 0:NT], qsl_d(g, 64, 128), [bQD[g]], [bcQM], bcQM)
                kts = [0, 1] if is_ctx else list(range(NKT))
                rK = [bKT[0]] if is_ctx else bKT
                rV = [bVA[0]] if is_ctx else bVA
                nk = len(kts)
                ob = 4
                yield
                for qt in range(NT // 128):
                    qsl = slice(qt * 128, (qt + 1) * 128)
                    pend = None

                    def pv(p):
                        for (gk, ppt, pkt, pki) in p:
                            mm(PS[ob + gk][:, :], VA[:, pkt, gk * 128:(gk + 1) * 128], cPT[ppt][:, :],
                               pki == 0, pki == nk - 1, rV + [bcPT[ppt]], [bPS[ob + gk]], sig=True)

                    for ki, kt in enumerate(kts):
                        cur = []
                        for gk in range(2):
                            sbank = (st["pslot"] % 2) * 2 + gk
                            mm(PS[sbank][:, :], KT[:, kt * 128:(kt + 1) * 128], cQM[gk][:, :, qsl], True, True,
                               rK + [bcQM], [bPS[sbank]], sig=True)
                            pt = (st["pslot"] % 2) * 2 + gk
                            act(cPT[pt][:, :], PS[sbank][:, :], AF.Exp, [bPS[sbank]], [bcPT[pt]], scale=0.125)
                            cur.append((gk, pt, kt, ki))
                        st["pslot"] += 1
                        if pend is not None:
                            pv(pend)
                        pend = cur
                        yield
                    pv(pend)
                    cp("dve", GV[:, :], PS[ob][:, :], [bPS[ob]], [bGV])
                    cp("dve", GV2[:, :], PS[ob + 1][:, :], [bPS[ob + 1]], [bGV2])
                    yield
                    S.op("dve", lambda e: e.reciprocal(out=RC[0:64, :], in_=GV[64:128, :]), [bGV], [bRC])
                    tt("dve", cATT[0:64, :, qsl], GV[0:64, :].rearrange("p (j q) -> p j q", j=4),
                       RC[0:64, :].rearrange("p (j q) -> p j q", j=4), ALU.mult, [bGV, bRC], [bcATT])
                    yield
                    S.op("dve", lambda e: e.reciprocal(out=RC[64:128, :], in_=GV2[0:64, :]), [bGV2], [bRC])
                    tt("dve", cATT[64:128, :, qsl], GV2[64:128, :].rearrange("p (j q) -> p j q", j=4),
                       RC[64:128, :].rearrange("p (j q) -> p j q", j=4), ALU.mult, [bGV2, bRC], [bcATT])
                    yield
                for o in range(8):
                    pi = o % 4
                    ps, bps = PS[pi], bPS[pi]
                    for j in range(4):
                        mm(ps[:, 0:NT], WOUT[:, j, o * 128:(o + 1) * 128], cATT[:, j, 0:NT], j == 0, j == 3,
                           [bWOUT, bcATT], [bps], sig=(j == 3))
                    residual(l, 2, v, o, ps, bps, xg, bxg, NT)
                    yield
                out.append((xg, bxg))

            def mlp_gen(idx, xg, bxg):
                g, (t0, NT, kp0, is_ctx) = glist[idx]
                v = 1 if is_ctx else 0
                dst, dstname = dsts[1] if is_ctx else dsts[0]
                res = []
                for _ in norm_gen(l, xg, bxg, NT, 3, v, res, ssb=6):
                    yield
                ht, bht = res[0]
                for j in range(8):
                    sl = st["wj1"] % 2
                    st["wj1"] += 1
                    S.dma("pool", cW1B[sl], w1_bf[l, j].rearrange("p (k n) -> p k n", n=512), [bWC[("w1", l)]], [bcW1B[sl]], bcW1B[sl])
                    for mi in range(4):
                        m = j * 4 + mi
                        pi = 6 + (m % 2)
                        ps, bps = PS[pi], bPS[pi]
                        for kc in range(8):
                            mm(ps[:, 0:NT], cW1B[sl][:, kc, mi * 128:(mi + 1) * 128], ht[:, kc, 0:NT], kc == 0, kc == 7,
                               [bcW1B[sl], bht], [bps], sig=(kc == 7))
                            if kc == 3:
                                yield
                        ti = 2 + m % 2
                        ts("dve", T_f[ti][:, 0:NT], ps[:, 0:NT], 0.0, None, ALU.max, None, [bps], [bTf[ti]])
                        tt("pool", cHID[:, m, 0:NT], T_f[ti][:, 0:NT], T_f[ti][:, 0:NT], ALU.mult, [bTf[ti]], [bcHID])
                        yield
                for o in range(8):
                    sl = st["wj2"] % 2
                    st["wj2"] += 1
                    S.dma("pool", cW2B[sl], w2_bf[l, o].rearrange("p (m c) -> p m c", c=128), [bWC[("w2", l)]], [bcW2B[sl]], bcW2B[sl])
                    pi = 6 + (o % 2)
                    ps, bps = PS[pi], bPS[pi]
                    for m in range(32):
                        mm(ps[:, 0:NT], cW2B[sl][:, m, :], cHID[:, m, 0:NT], m == 0, m == 31, [bcW2B[sl], bcHID], [bps], sig=(m == 31))
                        if m % 4 == 3:
                            yield
                    residual(l, 5, v, o, ps, bps, xg, bxg, NT)
                store_x(dst, dstname, g, t0, NT, xg, bxg)

            cur = []
            for _ in attn_gen(0, cur):
                pass
            for idx in range(len(glist)):
                xg, bxg = cur[0]
                nxt = []
                chains = [mlp_gen(idx, xg, bxg)]
                if idx + 1 < len(glist):
                    chains.insert(0, attn_gen(idx + 1, nxt))
                interleave(chains)
                cur = nxt

        S.barrier()
        if "nocast" not in debug:
            cast_queue.extend([(0, "a"), (0, "b")] + ([(1, "a"), (1, "b")] if nlayers > 1 else []))
            run_casts(1)
        stop_after = None
        for d_ in debug:
            if d_.startswith("stop:"):
                stop_after = d_[5:]
        cur_x, cur_c = (xT, "xT"), (ctxT, "ctxT")
        done = False
        for l in range(nlayers):
            if stop_after == "pro" or done:
                done = True
                break
            last = (l == DEPTH - 1)
            if "unmerged" in debug:
                seq = [("A", phase_A), ("B", phase_B), ("C", phase_C), ("D", phase_D)]
            else:
                seq = [("A", phase_A), ("B", phase_B), ("D", phase_CD)]
            for pi_, (pn, fn) in enumerate(seq):
                S.barrier()
                nx = (xb, "xb") if cur_x[1] == "xa" else (xa, "xa")
                ncx = (cb, "cb") if cur_c[1] == "ca" else (ca, "ca")
                if pn == "D" and l == nlayers - 1:
                    nx = (outT, "outT")
                try:
                    fn(l, (cur_x, cur_c), (nx, ncx))
                except _Stop:
                    done = True
                    cur_x = (outT, "outT")
                    break
                cur_x = nx
                if not last:
                    cur_c = ncx
                if stop_after == f"{l}{pn}":
                    done = True
                    break
            if done:
                break
        if done and cur_x[1] != "outT":
            for g in range(1, 9):
                t0 = (g - 1) * 512
                xg, bxg = load_x(cur_x[0], cur_x[1], g, t0, 512)
                store_x(outT, "outT", g, t0, 512, xg, bxg)
        if "ctxout" in debug:
            d = nc.dram_tensor("dbg_ctxout", [128, 8, CTX], F32, kind="ExternalOutput").ap()
            dbg_out["ctxout"] = d
            bXD[("dbgc", 0)] = B("dbgc0")
            xg, bxg = load_x(cur_c[0], cur_c[1], 0, 0, CTX)
            store_x(d, "dbgc", 0, 0, CTX, xg, bxg)
        if "kt" in debug:
            dbg_dump("kt", KT, [128, NTOK], BF16, bKT)
        if "va" in debug:
            dbg_dump("va", VA, [128, NKT, 256], BF16, bVA)
        if "ft" in debug:
            dbg_dump("ft", FT, [128, NKT, 256], BF16, bFT)
        if "qT" in debug:
            d = nc.dram_tensor("dbg_qT", [128, 4, NTOK], BF16, kind="ExternalOutput").ap()
            dbg_out["qT"] = d
            for i in range(2):
                S.dma("sp", QST[i][:, :, :], qsl_d(i + 1), bQD, [bQST[i]], bQST[i])
                S.dma("sp", d[:, :, 256 + i * 512:256 + (i + 1) * 512], QST[i][:, :, :], [bQST[i]], [], bQST[i])
        S.barrier()
        S.emit()
    return nc


_NC_CACHE = {}


def kernel(**inputs):
    inputs = {k: np.asarray(v) for k, v in inputs.items()}
    sh, cores = host_layout(inputs)
    if "nc" not in _NC_CACHE:
        _NC_CACHE["nc"] = build()
    nc = _NC_CACHE["nc"]
    in_maps = []
    for b in range(8):
        m = dict(sh)
        m.update(cores[b])
        in_maps.append(m)
    res = run_bass_kernel_spmd(nc, in_maps, core_ids=list(range(8)))
    out = np.stack([np.ascontiguousarray(np.asarray(res.results[b]["outT"]).transpose(0, 3, 2, 1).reshape(S_LAT, D))
                    for b in range(8)], axis=0)
    return out.astype(np.float32)
```

```python
import bisect
from contextlib import ExitStack

import numpy as np
import ml_dtypes

import concourse.bass as bass
import concourse.mybir as mybir
from concourse.bass_utils import run_bass_kernel_spmd

F32 = mybir.dt.float32
BF16 = mybir.dt.bfloat16
AF = mybir.ActivationFunctionType
ALU = mybir.AluOpType
AX = mybir.AxisListType

DEPTH = 2
D = 1024
S_LAT = 4096
CTX = 256
NTOK = S_LAT + CTX
EPS = 1e-6
NKT = NTOK // 128


class _Stop(Exception):
    pass


class Buf:
    __slots__ = ("name", "w", "r", "sem", "ndma", "nobar", "excl")

    def __init__(self, name):
        self.name = name
        self.nobar = False
        self.excl = False
        self.w = []
        self.r = []
        self.sem = None
        self.ndma = 0


class Eng:
    def __init__(self, name, sem):
        self.name = name
        self.sem = sem
        self.seq = 0
        self.sigseqs = []
        self.waited = {}
        self.prog = []


class Sched:
    def __init__(self, nc, es):
        self.nc = nc
        self.es = es
        self.E = {}
        for n in ("pe", "act", "dve", "pool", "sp"):
            self.E[n] = Eng(n, es.enter_context(nc.semaphore("e_" + n)))
        self.dbufs = []
        self.nsem = 5

    def _resolve(self, tok):
        if tok[0] == "c":
            e, seq = tok[1], tok[2]
            i = bisect.bisect_left(e.sigseqs, seq)
            assert i < len(e.sigseqs), "dependency on unsignaled op on " + e.name
            return e.sem, i + 1, e
        b = tok[1]
        return b.sem, 16 * b.ndma, None

    def _deps(self, eng, reads, writes):
        toks = []
        for b in reads:
            toks.extend(b.w)
        for b in writes:
            toks.extend(b.w)
            toks.extend(b.r)
        need = {}
        for t in toks:
            if t[0] == "c" and t[1] is eng and eng.name == "pe":
                continue
            sem, val, src = self._resolve(t)
            key = id(sem)
            if val > eng.waited.get(key, 0):
                if key not in need or need[key][1] < val:
                    need[key] = (sem, val)
        for key, (sem, val) in need.items():
            eng.prog.append(("wait", sem, val))
            eng.waited[key] = val

    def _mark(self, tok, reads, writes):
        wset = set(id(b) for b in writes)
        for b in reads:
            if id(b) in wset:
                continue
            if tok[0] == "c":
                b.r = [t for t in b.r if not (t[0] == "c" and t[1] is tok[1])]
            elif tok in b.r:
                continue
            b.r.append(tok)
        for b in writes:
            b.w = [tok]
            b.r = []

    def op(self, en, fn, reads, writes, sig=True):
        eng = self.E[en]
        if any(b.excl for b in reads):
            writes = list(writes) + [b for b in reads if b.excl]
            reads = [b for b in reads if not b.excl]
        self._deps(eng, reads, writes)
        eng.seq += 1
        if sig:
            eng.sigseqs.append(eng.seq)
        eng.prog.append(("ins", fn, sig))
        self._mark(("c", eng, eng.seq), reads, writes)

    def dma(self, q, out, in_, reads, writes, semb):
        eng = self.E[q]
        self._deps(eng, reads, writes)
        if semb.sem is None:
            semb.sem = self.es.enter_context(self.nc.semaphore("d_" + semb.name))
            self.nsem += 1
            self.dbufs.append(semb)
        sem = semb.sem
        eng.prog.append(("dma", out, in_, sem))
        semb.ndma += 1
        self._mark(("d", semb), reads, writes)

    def barrier(self):
        targets = []
        for e in self.E.values():
            if e.sigseqs:
                targets.append((e.sem, len(e.sigseqs)))
        for b in self.dbufs:
            if not b.nobar:
                targets.append((b.sem, 16 * b.ndma))
        for e in self.E.values():
            for sem, val in targets:
                key = id(sem)
                if val > e.waited.get(key, 0):
                    e.prog.append(("wait", sem, val))
                    e.waited[key] = val

    def emit(self):
        nc = self.nc
        attr = {"pe": "tensor", "act": "scalar", "dve": "vector", "pool": "gpsimd", "sp": "sync"}
        with nc.Block() as block:
            for n, eng in self.E.items():
                def body(e, eng=eng):
                    for it in eng.prog:
                        if it[0] == "wait":
                            e.wait_ge(it[1], it[2])
                        elif it[0] == "ins":
                            ins = it[1](e)
                            if it[2]:
                                ins.then_inc(eng.sem, 1)
                        else:
                            e.dma_start(out=it[1], in_=it[2]).then_inc(it[3], 16)
                getattr(block, attr[n])(body)


def _bf(a):
    return np.ascontiguousarray(np.asarray(a, dtype=np.float32).astype(ml_dtypes.bfloat16))


_CONST_CACHE = {}


def host_consts():
    if _CONST_CACHE:
        return _CONST_CACHE
    c = {}
    c["ones"] = _bf(np.ones((128, 128)))
    blk = np.zeros((128, 128), np.float32)
    blk[:64, :64] = 1.0
    blk[64:, 64:] = 1.0
    c["blk64"] = _bf(blk)
    perm = np.zeros((128, 128), np.float32)
    for m in range(128):
        d = m % 64
        pd = d + 16 if (d % 32) < 16 else d - 16
        perm[(m // 64) * 64 + pd, m] = 1.0
    c["perm"] = _bf(perm)
    inv = 10000.0 ** (-np.arange(0, 32, 2, dtype=np.float64) / 32.0)
    t = np.arange(S_LAT)
    row = (t // 64).astype(np.float64)
    col = (t % 64).astype(np.float64)
    cosT = np.zeros((128, S_LAT), np.float64)
    sinT = np.zeros((128, S_LAT), np.float64)
    for p in range(128):
        d = p % 64
        pos = row if d < 32 else col
        f = inv[d % 16]
        sgn = -1.0 if (d % 32) < 16 else 1.0
        cosT[p] = np.cos(pos * f)
        sinT[p] = sgn * np.sin(pos * f)
    c["cosT"] = _bf(cosT)
    c["sinT"] = _bf(sinT)
    k = np.arange(64)
    ang = 2 * np.pi * np.outer(k, k) / 64.0
    cc, sc = np.cos(ang), np.sin(ang)
    def bd(m):
        o = np.zeros((128, 128))
        o[:64, :64] = m
        o[64:, 64:] = m
        return o
    c["bdc_lat"] = _bf(bd(cc) / 512.0)
    c["bds_lat"] = _bf(-bd(sc) / 512.0)
    c["bdc_ctx"] = _bf(bd(cc) / 128.0)
    c["bds_ctx"] = _bf(-bd(sc) / 128.0)
    n = np.arange(S_LAT)
    m = (np.outer(n, n) % S_LAT).astype(np.float64)
    def relay(a):
        return np.ascontiguousarray(a.reshape(32, 128, 8, 512).transpose(2, 1, 0, 3))
    c["dftc"] = relay(_bf(np.cos(2 * np.pi * m / S_LAT)))
    c["dfts"] = relay(_bf(np.sin(2 * np.pi * m / S_LAT)))
    n2 = np.arange(CTX)
    m2 = (np.outer(n2, n2) % CTX).astype(np.float64)
    c["dftc256"] = _bf(np.cos(2 * np.pi * m2 / CTX))
    c["dfts256"] = _bf(np.sin(2 * np.pi * m2 / CTX))
    _CONST_CACHE.update(c)
    return c


def host_layout(inp):
    f32 = np.float32
    sh = {}
    L = DEPTH
    sh["w_ada"] = np.ascontiguousarray(inp["w_ada"].reshape(L, 8, 128, 24, 256).transpose(0, 3, 2, 1, 4), f32)
    sh["eye2"] = np.eye(2, dtype=f32)
    sh["b_adaT"] = np.ascontiguousarray(inp["b_ada"].reshape(L, 48, 128).transpose(0, 2, 1), f32)
    sh["g1T"] = np.ascontiguousarray(inp["norm1_g"].reshape(L, 8, 128).transpose(0, 2, 1), f32)
    sh["g2T"] = np.ascontiguousarray(inp["norm2_g"].reshape(L, 8, 128).transpose(0, 2, 1), f32)
    qcols = []
    for j in range(4):
        qcols += list(range(64 * j, 64 * j + 64)) + list(range(64 * (j + 4), 64 * (j + 4) + 64))
    kcols = list(range(512, 640))
    vcols = list(range(640, 768))
    fcols = list(range(768, 1024))
    ucols = list(range(1024, 1280))
    gvcols = list(range(1280, 1536))
    order = qcols + kcols + ucols + vcols + fcols + gvcols
    sh["w_in"] = np.ascontiguousarray(inp["w_in"][:, :, order], f32)
    gq = inp["q_norm_g"]
    gk = inp["k_norm_g"]
    gqk = np.zeros((L, 128, 2), f32)
    gqk[:, :, 0] = np.concatenate([gq, gq], axis=1)
    gqk[:, :, 1] = np.concatenate([gk, gk], axis=1)
    sh["gqk"] = gqk
    sh["vgb"] = np.ascontiguousarray(np.broadcast_to(inp["gmlp_v_g"][:, None, :], (L, 128, 256)), f32)
    sh["w_sT"] = np.ascontiguousarray(inp["w_spatial"].transpose(0, 3, 1, 2), f32)
    bs = inp["b_spatial"].reshape(L, 2, 2, 128)
    bsb = np.broadcast_to(bs.transpose(0, 2, 1, 3)[:, :, None, :, :], (L, 2, 64, 2, 128))
    sh["bsb"] = np.ascontiguousarray(bsb.reshape(L, 128, 2, 128), f32)
    rorder = qcols + list(range(512, 1024))
    sh["w_out"] = np.ascontiguousarray(inp["w_out"][:, rorder, :], f32)
    sh["w1r"] = np.ascontiguousarray(inp["w_mlp1"].reshape(L, 8, 128, 8, 512).transpose(0, 3, 2, 1, 4), f32)
    sh["w2r"] = np.ascontiguousarray(inp["w_mlp2"].reshape(L, 32, 128, 8, 128).transpose(0, 3, 2, 1, 4), f32)
    sh.update(host_consts())
    cores = []
    B = inp["x"].shape[0]
    for b in range(B):
        d = {}
        d["xT"] = np.ascontiguousarray(inp["x"][b].reshape(8, 512, 8, 128).transpose(0, 3, 2, 1), f32)
        d["ctxT"] = np.ascontiguousarray(inp["ctx"][b].reshape(CTX, 8, 128).transpose(2, 1, 0), f32)
        ccv = np.zeros((128, 8, 2), f32)
        ccv[:, :, 0] = inp["c"][b].reshape(8, 128).T
        ccv[:, :, 1] = inp["c_ctx"].reshape(8, 128).T
        d["cc"] = ccv
        cores.append(d)
    return sh, cores


def build(nlayers=DEPTH, debug=()):
    nc = bass.Bass("TRN2", target_bir_lowering=False)
    es = ExitStack()
    with es:
        S = Sched(nc, es)

        def din(name, shape, dt=F32):
            return nc.dram_tensor(name, list(shape), dt, kind="ExternalInput").ap()

        def dscr(name, shape, dt):
            return nc.dram_tensor(name, list(shape), dt, kind="Internal").ap()

        xT = din("xT", [8, 128, 8, 512])
        ctxT = din("ctxT", [128, 8, CTX])
        cc_d = din("cc", [128, 8, 2])
        w_ada_d = din("w_ada", [DEPTH, 24, 128, 8, 256])
        eye2_d = din("eye2", [2, 2], F32)
        b_adaT_d = din("b_adaT", [DEPTH, 128, 48])
        g1T_d = din("g1T", [DEPTH, 128, 8])
        g2T_d = din("g2T", [DEPTH, 128, 8])
        w_in_d = din("w_in", [DEPTH, D, 1536])
        gqk_d = din("gqk", [DEPTH, 128, 2])
        vgb_d = din("vgb", [DEPTH, 128, 256])
        w_sT_d = din("w_sT", [DEPTH, 128, 4, 128])
        bsb_d = din("bsb", [DEPTH, 128, 2, 128])
        w_out_d = din("w_out", [DEPTH, D, D])
        w1r_d = din("w1r", [DEPTH, 8, 128, 8, 512])
        w2r_d = din("w2r", [DEPTH, 8, 128, 32, 128])
        cst = {}
        for nm in ("ones", "blk64", "perm", "bdc_lat", "bds_lat", "bdc_ctx", "bds_ctx"):
            cst[nm] = din(nm, [128, 128], BF16)
        cosT_d = din("cosT", [128, S_LAT], BF16)
        sinT_d = din("sinT", [128, S_LAT], BF16)
        dftc_d = din("dftc", [8, 128, 32, 512], BF16)
        dfts_d = din("dfts", [8, 128, 32, 512], BF16)
        dftc256_d = din("dftc256", [CTX, CTX], BF16)
        dfts256_d = din("dfts256", [CTX, CTX], BF16)

        outT = nc.dram_tensor("outT", [8, 128, 8, 512], F32, kind="ExternalOutput").ap()
        dbg_out = {}

        xa = dscr("xa", [8, 128, 8, 512], F32)
        xb = dscr("xb", [8, 128, 8, 512], F32)
        ca = dscr("ca", [128, 8, CTX], F32)
        cb = dscr("cb", [128, 8, CTX], F32)
        qT_lat = dscr("qT_lat", [8, 128, 4, 512], BF16)
        qT_ctx = dscr("qT_ctx", [128, 4, CTX], BF16)

        def qsl_d(g, lo=0, hi=128):
            return qT_ctx[lo:hi, :, :] if g == 0 else qT_lat[g - 1, lo:hi, :, :]
        w_in_bf = dscr("w_in_bf", [DEPTH, D, 1536], BF16)
        w_out_bf = dscr("w_out_bf", [DEPTH, D, D], BF16)
        w1_bf = dscr("w1_bf", [DEPTH, 8, 128, 8 * 512], BF16)
        w2_bf = dscr("w2_bf", [DEPTH, 8, 128, 32 * 128], BF16)

        def sb(name, shape, dt):
            return es.enter_context(nc.sbuf_tensor(name, list(shape), dt))

        UN = 34816
        U = sb("U", [128, UN], BF16)
        KT = U[:, 0:NTOK]
        VA = U[:, NTOK:NTOK + NKT * 256].rearrange("p (t c) -> p t c", c=256)
        o2 = NTOK + NKT * 256
        FT = U[:, o2:o2 + NKT * 256].rearrange("p (t c) -> p t c", c=256)
        o3 = o2 + NKT * 256
        COS = U[:, o3:o3 + S_LAT]
        SIN = U[:, o3 + S_LAT:o3 + 2 * S_LAT]
        assert o3 + 2 * S_LAT <= UN
        HID = U[:, 0:16384].rearrange("p (m t) -> p m t", t=512)
        W1B = [U[:, 16384 + i * 4096:16384 + (i + 1) * 4096].rearrange("p (k n) -> p k n", n=512) for i in range(2)]
        W2B = [U[:, 24576 + i * 4096:24576 + (i + 1) * 4096].rearrange("p (m c) -> p m c", c=128) for i in range(2)]
        WAB = [U[:, i * 6144:(i + 1) * 6144].rearrange("p (k f) -> p k f", f=768) for i in range(2)]

        WIN = sb("WIN", [128, 8, 1536], BF16)
        WOUT = sb("WOUT", [128, 8, 1024], BF16)
        XG = [sb(f"XG{i}", [128, 8, 512], F32) for i in range(2)]
        HT = [sb(f"HT{i}", [128, 8, 512], BF16) for i in range(2)]
        DFB = [[HT[i][:, 4 * ci:4 * ci + 4, :] for i in range(2)] for ci in range(2)]
        NDF = 5
        DF256 = None
        CST = {nm: sb("c_" + nm, [128, 128], BF16) for nm in cst}
        EYE2 = sb("EYE2", [2, 2], F32)
        P_bada = sb("P_bada", [128, DEPTH, 48], F32)
        P_g1 = sb("P_g1", [128, DEPTH, 8], F32)
        P_g2 = sb("P_g2", [128, DEPTH, 8], F32)
        P_gqk = sb("P_gqk", [128, DEPTH, 2], F32)
        P_vgb = sb("P_vgb", [128, DEPTH, 256], F32)
        P_wsT = sb("P_wsT", [128, DEPTH, 4, 128], BF16)
        P_bsb = sb("P_bsb", [128, DEPTH, 2, 128], F32)
        P_cc = sb("P_cc", [128, 8, 2], F32)
        SC = sb("SC", [128, 8, 2], BF16)
        MOD = sb("MOD", [128, 48, 2], F32)
        VEC = sb("VEC", [128, DEPTH, 6, 8, 2], F32)
        TAf = sb("TAf", [128, 5120], F32)
        T_f = [TAf[:, i * 512:(i + 1) * 512] for i in range(6)]
        RSTD = TAf[:, 3072:3584]
        GV = TAf[:, 3584:4096]
        GV2 = TAf[:, 4096:4608]
        RC = TAf[:, 4608:5120]
        SSV = sb("SSV", [128, 8], F32)
        RV = sb("RV", [128, 8], F32)
        TAb = sb("TAb", [128, 9728], BF16)
        T_b = [TAb[:, i * 512:(i + 1) * 512] for i in range(6)]
        QST = [TAb[:, 3072 + i * 2048:3072 + (i + 1) * 2048].rearrange("p (j t) -> p j t", t=512) for i in range(2)]
        GUT = TAb[:, 7168:8192].rearrange("p (c t) -> p c t", t=512)
        GMT = TAb[:, 8192:9216].rearrange("p (c t) -> p c t", t=512)
        GMTB = sb("GMTB", [128, 2, 512], BF16)
        DF256 = [TAb[:, 5120 + i * 512:5120 + (i + 1) * 512].rearrange("p (n k) -> p n k", k=256) for i in range(2)]
        GMT2 = [GMT, GMTB]
        VHN = TAb[:, 9216:9728]
        AB = [TAb[:, i * 512:(i + 1) * 512] for i in range(4)]
        YT = TAb[:, 3072:4096].rearrange("p (c t) -> p c t", t=512)
        WINf = WIN[:].rearrange("p k n -> p (k n)")
        QM = [[WINf[:, g * 2048:(g + 1) * 2048].rearrange("p (j t) -> p j t", t=512) for g in range(2)]]
        PT = [WINf[:, 4096 + i * 1024:4096 + (i + 1) * 1024] for i in range(3)]
        ATT = WINf[:, 7168:9216].rearrange("p (j t) -> p j t", t=512)
        assert 4096 + 3 * 1024 <= 7168
        PTH = [PT[i // 2][:, (i % 2) * 512:(i % 2 + 1) * 512] for i in range(6)]
        cHID = U[:, 13056:29440].rearrange("p (m t) -> p m t", t=512)
        cW2B = [U[:, 29440:33536].rearrange("p (m c) -> p m c", c=128),
                WINf[:, 8192:12288].rearrange("p (m c) -> p m c", c=128)]
        cW1B = [WINf[:, i * 4096:(i + 1) * 4096].rearrange("p (k n) -> p k n", n=512) for i in range(2)]
        cQM = [TAb[:, 3072 + g * 2048:3072 + (g + 1) * 2048].rearrange("p (j t) -> p j t", t=512) for g in range(2)]
        cPT = [TAb[:, 7168 + i * 512:7168 + (i + 1) * 512] for i in range(4)]
        WOUTf = WOUT[:].rearrange("p k n -> p (k n)")
        cATT = WOUTf[:, 4096:6144].rearrange("p (j t) -> p j t", t=512)
        for ci in range(2):
            for i in range(3):
                off = (i * 2 + ci) * 2048
                DFB[ci].append(WINf[:, off:off + 2048].rearrange("p (n k) -> p n k", k=512))

        PSW = [es.enter_context(nc.psum_tensor(f"psw{i}", [128, 1024], F32)) for i in range(4)]
        PS = [PSW[i // 2][:, (i % 2) * 512:(i % 2 + 1) * 512] for i in range(8)]

        def B(n):
            return Buf(n)

        bU = B("U")
        bKT = [B(f"KT{g}") for g in range(9)]
        bVA = [B(f"VA{g}") for g in range(9)]
        bFT = [B(f"FT{g}") for g in range(9)]
        bTAB = B("TAB")
        bHID = B("HID")
        bW1B = [B(f"W1B{i}") for i in range(2)]
        bW2B = [B(f"W2B{i}") for i in range(2)]
        bWAB = [B(f"WAB{i}") for i in range(2)]
        bWIN, bWOUT = B("WIN"), B("WOUT")
        bXG = [B(f"XG{i}") for i in range(2)]
        bHT = [B(f"HT{i}") for i in range(2)]
        bDFB = [[B(f"DF{cs}{i}") for i in range(5)] for cs in "cs"]
        bDF256 = B("DF256")
        bCST = B("CST")
        bPAR = B("PAR")
        bPARW = B("PARW")
        bSC, bMOD, bVEC = B("SC"), B("MOD"), B("VEC")
        bTf = [B(f"Tf{i}") for i in range(6)]
        bTb = [B(f"Tb{i}") for i in range(6)]
        bRSTD = B("RSTD")
        bQST = [B(f"QST{i}") for i in range(2)]
        bQM = [B("QM0")]
        bGUT, bGMT, bGV, bGV2, bVHN, bSSV, bRV = B("GUT"), B("GMT"), B("GV"), B("GV2"), B("VHN"), B("SSV"), B("RV")
        bGMT2 = [bGMT, B("GMTB")]
        bPT = [B(f"PT{i}") for i in range(3)]
        bPTH = [B(f"PTH{i}") for i in range(6)]
        bcHID, bcQM, bcATT = B("cHID"), B("cQM"), B("cATT")
        bcW1B = [B(f"cW1B{i}") for i in range(2)]
        bcW2B = [B(f"cW2B{i}") for i in range(2)]
        bcPT = [B(f"cPT{i}") for i in range(4)]
        bATT, bRC = B("ATT"), B("RC")
        bAB = [B(f"AB{i}") for i in range(4)]
        bYT = B("YT")
        bPS = [B(f"PS{i}") for i in range(8)]
        for b_ in bPS:
            b_.excl = True
        bXD = {}
        for nm in ("xT", "xa", "xb", "outT", "ctxT", "ca", "cb"):
            for g in range(9):
                bXD[(nm, g)] = B(f"{nm}{g}")
        bQD = [B(f"QD{g}") for g in range(9)]
        bWC = {}
        for nm in ("w_in", "w_out", "w1", "w2"):
            for l in range(DEPTH):
                bWC[(nm, l)] = B(f"WC{nm}{l}")
                bWC[(nm, l)].nobar = True

        def mm(out, lhsT, rhs, start, stop, r, w, sig=False):
            S.op("pe", lambda e: e.matmul(out, lhsT=lhsT, rhs=rhs, start=start, stop=stop), r, w, sig)

        def act(out, in_, func, r, w, bias=None, scale=None):
            kw = {}
            if bias is not None:
                kw["bias"] = bias
            if scale is not None:
                kw["scale"] = scale
            S.op("act", lambda e: e.activation(out=out, in_=in_, func=func, **kw), r, w)

        def tt(en, out, in0, in1, op, r, w):
            S.op(en, lambda e: e.tensor_tensor(out=out, in0=in0, in1=in1, op=op), r, w)

        def ts(en, out, in0, s1, s2, op0, op1, r, w):
            if op1 is None:
                S.op(en, lambda e: e.tensor_scalar(out=out, in0=in0, scalar1=s1, scalar2=None, op0=op0), r, w)
            else:
                S.op(en, lambda e: e.tensor_scalar(out=out, in0=in0, scalar1=s1, scalar2=s2, op0=op0, op1=op1), r, w)

        def stt(out, in0, scalar, in1, op0, op1, r, w):
            S.op("dve", lambda e: e.scalar_tensor_tensor(out=out, in0=in0, scalar=scalar, in1=in1, op0=op0, op1=op1), r, w)

        def cp(en, out, in_, r, w):
            if en == "act":
                S.op(en, lambda e: e.activation(out=out, in_=in_, func=AF.Identity), r, w)
            else:
                S.op(en, lambda e: e.tensor_copy(out=out, in_=in_), r, w)

        def ckpt(name):
            if ("ck:" + name) in debug:
                raise _Stop()

        def dbg_dump(name, ap, shape, dt, bufs):
            if name not in debug:
                return
            d = nc.dram_tensor("dbg_" + name, list(shape), dt, kind="ExternalOutput").ap()
            dbg_out[name] = d
            S.dma("sp", d, ap, bufs, [], bufs[0])

        for nm in cst:
            S.dma("sp", CST[nm][:], cst[nm][:, :], [], [bCST], bCST)
        S.dma("sp", EYE2[:], eye2_d[:, :], [], [bCST], bCST)
        S.dma("sp", P_cc[:], cc_d[:, :, :], [], [bPAR], bPAR)
        for l in range(DEPTH):
            S.dma("sp", P_bada[:, l, :], b_adaT_d[l], [], [bPAR], bPAR)
            S.dma("sp", P_g1[:, l, :], g1T_d[l], [], [bPAR], bPAR)
            S.dma("sp", P_g2[:, l, :], g2T_d[l], [], [bPAR], bPAR)
            S.dma("sp", P_gqk[:, l, :], gqk_d[l], [], [bPAR], bPAR)
            S.dma("sp", P_vgb[:, l, :], vgb_d[l], [], [bPAR], bPAR)
            S.dma("sp", P_bsb[:, l, :, :], bsb_d[l], [], [bPAR], bPAR)
            S.dma("pool", P_wsT[:, l, :, :], w_sT_d[l], [], [bPARW], bPARW)

        act(SC[:], P_cc[:], AF.Silu, [bPAR], [bSC])

        WAB3 = [U[:, i * 4096:(i + 1) * 4096].rearrange("p (k f) -> p k f", f=512) for i in range(3)]
        bWAB3 = [B(f"WAB3_{i}") for i in range(3)]
        wslot = 0
        STG = [XG[i // 2][:, :, (i % 2) * 256:(i % 2 + 1) * 256] for i in range(4)]
        bSTG = [B(f"STG{i}") for i in range(4)]
        bWABp = [[B(f"WABp{i}_{p_}") for p_ in range(3)] for i in range(3)]
        parts = [("dve", 0, 4), ("act", 4, 7), ("pool", 7, 8)]
        for l in range(nlayers):
            pend = None

            def transposes(p):
                pj, ti = p
                for fc in range(2):
                    ch = pj * 2 + fc
                    mm(PS[0][:, ch * 2:ch * 2 + 2], T_f[ti][0:2, fc * 128:(fc + 1) * 128], EYE2[:, :], True, True,
                       [bTf[ti], bCST], [bPS[0]], sig=(fc == 1))

            for j in range(24):
                sl = wslot % 4
                wl = wslot % 3
                wslot += 1
                S.dma("sp", STG[sl], w_ada_d[l, j], [], [bSTG[sl]], bSTG[sl])
                wv = WAB3[wl][:, :, 0:256]
                for pi_, (en, lo, hi) in enumerate(parts):
                    cp(en, wv[:, lo:hi, :], STG[sl][:, lo:hi, :], [bSTG[sl]], [bWABp[wl][pi_]])
                pi = 1 + j % 2
                for dk in range(8):
                    pb = bWABp[wl][0 if dk < 4 else (1 if dk < 7 else 2)]
                    mm(PS[pi][0:2, 0:256], SC[:, dk, :], wv[:, dk, :], dk == 0, dk == 7, [pb, bSC], [bPS[pi]], sig=True)
                ti = j % 2
                cp("dve", T_f[ti][0:2, 0:256], PS[pi][0:2, 0:256], [bPS[pi]], [bTf[ti]])
                if pend is not None:
                    transposes(pend)
                pend = (j, ti)
            transposes(pend)
            tt("dve", MOD[:], PS[0][:, 0:96].rearrange("p (c v) -> p c v", v=2),
               P_bada[:, l, :].unsqueeze(2).broadcast_to([128, 48, 2]), ALU.add, [bPS[0], bPAR], [bMOD])
            stt(VEC[:, l, 0, :, :], MOD[:, 8:16, :], 1.0, P_g1[:, l, :].unsqueeze(2).broadcast_to([128, 8, 2]),
                ALU.add, ALU.mult, [bMOD, bPAR], [bVEC])
            cp("dve", VEC[:, l, 1, :, :], MOD[:, 0:8, :], [bMOD], [bVEC])
            cp("dve", VEC[:, l, 2, :, :], MOD[:, 16:24, :], [bMOD], [bVEC])
            stt(VEC[:, l, 3, :, :], MOD[:, 32:40, :], 1.0, P_g2[:, l, :].unsqueeze(2).broadcast_to([128, 8, 2]),
                ALU.add, ALU.mult, [bMOD, bPAR], [bVEC])
            cp("dve", VEC[:, l, 4, :, :], MOD[:, 24:32, :], [bMOD], [bVEC])
            cp("dve", VEC[:, l, 5, :, :], MOD[:, 40:48, :], [bMOD], [bVEC])
        dbg_dump("vec", VEC[:], [128, DEPTH, 6, 8, 2], F32, [bVEC])

        def cast_w(l, which):
            if which == "b":
                S.dma("pool", w1_bf[l].rearrange("j p (a n) -> (j p a) n", n=2048),
                      w1r_d[l].rearrange("j p k n -> (j p) (k n)").rearrange("r (a n) -> (r a) n", n=2048),
                      [], [bWC[("w1", l)]], bWC[("w1", l)])
                S.dma("pool", w2_bf[l].rearrange("o p (a n) -> (o p a) n", n=2048),
                      w2r_d[l].rearrange("o p m c -> (o p) (m c)").rearrange("r (a n) -> (r a) n", n=2048),
                      [], [bWC[("w2", l)]], bWC[("w2", l)])
                return
            S.dma("pool", w_in_bf[l].rearrange("k (a n) -> (k a) n", n=768),
                  w_in_d[l].rearrange("k (a n) -> (k a) n", n=768), [], [bWC[("w_in", l)]], bWC[("w_in", l)])
            S.dma("pool", w_out_bf[l], w_out_d[l], [], [bWC[("w_out", l)]], bWC[("w_out", l)])

        cast_queue = []

        def run_casts(n=100):
            while cast_queue and n > 0:
                l_, w_ = cast_queue.pop(0)
                cast_w(l_, w_)
                n -= 1

        groups = [(0, CTX, 0, True)] + [(i * 512, 512, CTX + i * 512, False) for i in range(8)]

        def xsrc(ap, t0, NT):
            if NT == CTX:
                return ap[:, :, :]
            return ap[t0 // 512]

        state = {"xslot": 0, "hslot": 0, "psr": 0, "dfslot": 0}

        def interleave(gens):
            gens = list(gens)
            while gens:
                for gen in list(gens):
                    try:
                        next(gen)
                    except StopIteration:
                        gens.remove(gen)

        def norm_gen(l, xg, bxg, NT, which, v, out, ssb=0):
            hs = state["hslot"] % 2
            state["hslot"] += 1
            for kc in range(8):
                si = 2 + kc % 4
                en = ("pool", "act", "dve", "act")[kc % 4]
                if en == "act":
                    act(T_b[si][:, 0:NT], xg[:, kc, 0:NT], AF.Square, [bxg], [bTb[si]])
                else:
                    tt(en, T_b[si][:, 0:NT], xg[:, kc, 0:NT], xg[:, kc, 0:NT], ALU.mult, [bxg], [bTb[si]])
                mm(PS[ssb][:, 0:NT], CST["ones"][:], T_b[si][:, 0:NT], kc == 0, kc == 7, [bCST, bTb[si]], [bPS[ssb]], sig=True)
                if kc % 4 == 3:
                    yield
            act(RSTD[:, 0:NT], PS[ssb][:, 0:NT], AF.Ln, [bPS[ssb]], [bRSTD], bias=EPS, scale=1.0 / D)
            act(RSTD[:, 0:NT], RSTD[:, 0:NT], AF.Exp, [bRSTD], [bRSTD], scale=-0.5)
            yield
            for kc in range(8):
                ti = kc % 2
                tt("dve", T_f[ti][:, 0:NT], xg[:, kc, 0:NT], RSTD[:, 0:NT], ALU.mult, [bxg, bRSTD], [bTf[ti]])
                if kc % 3 == 2:
                    ts("pool", HT[hs][:, kc, 0:NT], T_f[ti][:, 0:NT], VEC[:, l, which, kc, v:v + 1],
                       VEC[:, l, which + 1, kc, v:v + 1], ALU.mult, ALU.add, [bTf[ti], bVEC], [bHT[hs]])
                else:
                    act(HT[hs][:, kc, 0:NT], T_f[ti][:, 0:NT], AF.Identity, [bTf[ti], bVEC], [bHT[hs]],
                        bias=VEC[:, l, which + 1, kc, v:v + 1], scale=VEC[:, l, which, kc, v:v + 1])
                yield
            out.append((HT[hs], bHT[hs]))

        def norm_mod(l, g, xg, bxg, NT, which, v):
            out = []
            for _ in norm_gen(l, xg, bxg, NT, which, v, out):
                pass
            return out[0]

        def load_x(src, srcname, g, t0, NT):
            xs = state["xslot"] % 2
            state["xslot"] += 1
            S.dma("sp", XG[xs][:, :, 0:NT], xsrc(src, t0, NT), [bXD[(srcname, g)]], [bXG[xs]], bXG[xs])
            return XG[xs], bXG[xs]

        def store_x(dst, dstname, g, t0, NT, xg, bxg):
            S.dma("sp", xsrc(dst, t0, NT), xg[:, :, 0:NT], [bxg], [bXD[(dstname, g)]], bxg)

        def residual(l, which, v, o, ps, bps, xg, bxg, NT):
            stt(xg[:, o, 0:NT], ps[:, 0:NT], VEC[:, l, which, o, v:v + 1], xg[:, o, 0:NT], ALU.mult, ALU.add,
                [bps, bVEC, bxg], [bxg])

        def phase_A(l, srcs, dsts):
            last = (l == DEPTH - 1)
            S.dma("sp", WIN[:], w_in_bf[l].rearrange("(c p) n -> p c n", p=128), [bWC[("w_in", l)]], [bWIN], bWIN)
            S.dma("sp", WOUT[:], w_out_bf[l].rearrange("(c p) n -> p c n", p=128), [bWC[("w_out", l)]], [bWOUT], bWOUT)
            S.dma("sp", COS, cosT_d[:, :], [], [bTAB], bTAB)
            S.dma("sp", SIN, sinT_d[:, :], [], [bTAB], bTAB)
            S.op("pool", lambda e: e.memset(VA[:, :, 64:192], 1.0), [], bVA)

            def fm_mm(ci, ht, bht, NT):
                pi = 1 + (state["psr"] % 2)
                state["psr"] += 1
                ps, bps = PS[pi], bPS[pi]
                for kc in range(8):
                    mm(ps[:, 0:NT], WIN[:, kc, ci * 128:(ci + 1) * 128], ht[:, kc, 0:NT], kc == 0, kc == 7,
                       [bWIN, bht], [bps], sig=(kc == 7))
                return ps, bps

            def fm_epi(g, kind, ci, ps, bps, NT, t0, kp0, is_ctx, qs):
                if kind == "u":
                    act(GUT[:, ci - 5, 0:NT], ps[:, 0:NT], AF.Gelu_apprx_tanh, [bps], [bGUT])
                    return
                gcol = 0 if kind == "q" else 1
                if kind == "q":
                    dest, bdest = QST[qs][:, ci, 0:NT], bQST[qs]
                else:
                    dest, bdest = KT[:, kp0:kp0 + NT], bKT[g]
                act(T_b[0][:, 0:NT], ps[:, 0:NT], AF.Square, [bps], [bTb[0]])
                ts("dve", T_b[1][:, 0:NT], ps[:, 0:NT], P_gqk[:, l, gcol:gcol + 1], None, ALU.mult, None, [bps, bPAR], [bTb[1]])
                yield
                mm(PS[3][:, 0:NT], CST["blk64"][:], T_b[0][:, 0:NT], True, True, [bCST, bTb[0]], [bPS[3]], sig=True)
                if not is_ctx:
                    mm(PS[4][:, 0:NT], CST["perm"][:], T_b[1][:, 0:NT], True, True, [bCST, bTb[1]], [bPS[4]], sig=True)
                yield
                act(T_f[2][:, 0:NT], PS[3][:, 0:NT], AF.Ln, [bPS[3]], [bTf[2]], bias=EPS, scale=1.0 / 64)
                if not is_ctx:
                    tt("dve", T_f[3][:, 0:NT], T_b[1][:, 0:NT], COS[:, t0:t0 + NT], ALU.mult, [bTb[1], bTAB], [bTf[3]])
                yield
                act(T_f[2][:, 0:NT], T_f[2][:, 0:NT], AF.Exp, [bTf[2]], [bTf[2]], scale=-0.5)
                if is_ctx:
                    yield
                    tt("dve", dest, T_b[1][:, 0:NT], T_f[2][:, 0:NT], ALU.mult, [bTb[1], bTf[2]], [bdest])
                    return
                tt("dve", T_f[4][:, 0:NT], PS[4][:, 0:NT], SIN[:, t0:t0 + NT], ALU.mult, [bPS[4], bTAB], [bTf[4]])
                yield
                tt("pool", T_f[3][:, 0:NT], T_f[3][:, 0:NT], T_f[4][:, 0:NT], ALU.add, [bTf[3], bTf[4]], [bTf[3]])
                yield
                tt("dve", dest, T_f[3][:, 0:NT], T_f[2][:, 0:NT], ALU.mult, [bTf[3], bTf[2]], [bdest])

            def fm_chain(g, chunks, ht, bht, NT, t0, kp0, is_ctx, qs, full):
                nxt = fm_mm(chunks[0][1], ht, bht, NT)
                for i, (kind, ci) in enumerate(chunks):
                    ps, bps = nxt
                    if i + 1 < len(chunks):
                        nxt = fm_mm(chunks[i + 1][1], ht, bht, NT)
                    yield
                    for _ in fm_epi(g, kind, ci, ps, bps, NT, t0, kp0, is_ctx, qs):
                        yield
                    yield
                if full:
                    S.dma("sp", qsl_d(g), QST[qs][:, :, 0:NT], [bQST[qs]], [bQD[g]], bQST[qs])

            def tok_chain(g, ht, bht, NT, kp0, full):
                ntile = NT // 128
                ncol = 384 if full else 128
                for tl in range(ntile):
                    kt = kp0 // 128 + tl
                    tsl = slice(tl * 128, (tl + 1) * 128)
                    for kc in range(8):
                        mm(PS[5][:, 0:ncol], ht[:, kc, tsl], WIN[:, kc, 896:896 + ncol], kc == 0, kc == 7,
                           [bWIN, bht], [bPS[5]], sig=(kc == 7))
                    yield
                    cp("dve", VA[:, kt, 0:64], PS[5][:, 0:64], [bPS[5]], [bVA[g]])
                    cp("dve", VA[:, kt, 192:256], PS[5][:, 64:128], [bPS[5]], [bVA[g]])
                    if full:
                        cp("act", FT[:, kt, :], PS[5][:, 128:384], [bPS[5]], [bFT[g]])
                    yield
                if not full:
                    return
                for bt in range(ntile // 2):
                    for t in range(2):
                        tsl = slice((bt * 2 + t) * 128, (bt * 2 + t + 1) * 128)
                        for kc in range(8):
                            mm(PS[6][:, t * 256:(t + 1) * 256], ht[:, kc, tsl], WIN[:, kc, 1280:1536], kc == 0, kc == 7,
                               [bWIN, bht], [bPS[6]], sig=(kc == 7))
                    yield
                    act(GV[:], PS[6][:, :], AF.Gelu_apprx_tanh, [bPS[6]], [bGV])
                    yield
                    tt("pool", GV2[:], GV[:], GV[:], ALU.mult, [bGV], [bGV2])
                    yield
                    S.op("dve", lambda e: e.tensor_reduce(out=SSV[:], in_=GV2[:].rearrange("p (h d) -> p h d", d=64),
                                                         axis=AX.X, op=ALU.add), [bGV2], [bSSV])
                    yield
                    act(RV[:], SSV[:], AF.Ln, [bSSV], [bRV], bias=EPS, scale=1.0 / 64)
                    act(RV[:], RV[:], AF.Exp, [bRV], [bRV], scale=-0.5)
                    yield
                    tt("dve", GV2[:].rearrange("p (h d) -> p h d", d=64), GV[:].rearrange("p (h d) -> p h d", d=64),
                       RV[:, 0:8].unsqueeze(2).broadcast_to([128, 8, 64]), ALU.mult, [bGV, bRV], [bGV2])
                    yield
                    tt("pool", VHN[:].rearrange("p (t c) -> p t c", c=256), GV2[:].rearrange("p (t c) -> p t c", c=256),
                       P_vgb[:, l, :].unsqueeze(1).broadcast_to([128, 2, 256]), ALU.mult, [bGV2, bPAR], [bVHN])
                    yield
                    ps7 = PS[6][:, :].rearrange("p (c t q) -> p c t q", c=2, t=2)
                    for t in range(2):
                        for c in range(2):
                            for hl in range(2):
                                h = 2 * c + hl
                                mm(ps7[hl * 64:(hl + 1) * 64, c, t, :], VHN[:, t * 256 + h * 64:t * 256 + (h + 1) * 64],
                                   P_wsT[:, l, h, :], True, True, [bVHN, bPARW], [bPS[6]], sig=(t == 1 and c == 1 and hl == 1))
                    yield
                    t5 = T_f[5][:, :].rearrange("p (c t q) -> p c t q", c=2, t=2)
                    tt("dve", t5, ps7, P_bsb[:, l, :, :].unsqueeze(2).broadcast_to([128, 2, 2, 128]), ALU.add,
                       [bPS[6], bPAR], [bTf[5]])
                    yield
                    bsl = slice(bt * 256, (bt + 1) * 256)
                    tt("dve", GMT2[g % 2][:, :, bsl], T_f[5][:, :].rearrange("p (c n) -> p c n", c=2), GUT[:, :, bsl], ALU.mult,
                       [bTf[5], bGUT], [bGMT2[g % 2]])
                    yield

            def prep_gen(gi, out, delay=0):
                t0_, NT_, kp0_, is_ctx_ = groups[gi]
                src_, srcname_ = srcs[1] if is_ctx_ else srcs[0]
                for _ in range(delay):
                    yield
                xg_, bxg_ = load_x(src_, srcname_, gi, t0_, NT_)
                res = []
                yield
                yield
                for _ in norm_gen(l, xg_, bxg_, NT_, 0, 1 if is_ctx_ else 0, res):
                    yield
                out.append((xg_, bxg_) + res[0])

            def tail_gen(g, v, NT, t0, xg, bxg, dst, dstname):
                ps, bps = PS[7], bPS[7]
                for o in range(8):
                    for c in range(2):
                        mm(ps[:, 0:NT], WOUT[:, 6 + c, o * 128:(o + 1) * 128], GMT2[g % 2][:, c, 0:NT], c == 0, c == 1,
                           [bWOUT, bGMT2[g % 2]], [bps], sig=(c == 1))
                    residual(l, 2, v, o, ps, bps, xg, bxg, NT)
                    yield
                store_x(dst, dstname, g, t0, NT, xg, bxg)

            tail_pending = None
            cur = []
            for _ in prep_gen(0, cur):
                pass
            for g, (t0, NT, kp0, is_ctx) in enumerate(groups):
                v = 1 if is_ctx else 0
                dst, dstname = dsts[1] if is_ctx else dsts[0]
                full = not (is_ctx and last)
                xg, bxg, ht, bht = cur[0]
                nxt = []
                qs = g % 2
                chunks = []
                if full:
                    chunks += [("u", 5), ("u", 6)]
                    chunks += [("q", j) for j in range(4)]
                chunks.append(("k", 4))
                chains = [fm_chain(g, chunks, ht, bht, NT, t0, kp0, is_ctx, qs, full),
                          tok_chain(g, ht, bht, NT, kp0, full)]
                delay = 0
                if tail_pending is not None:
                    chains.insert(0, tail_pending)
                    tail_pending = None
                    delay = 9
                if g + 1 < len(groups):
                    chains.append(prep_gen(g + 1, nxt, delay))
                interleave(chains)
                cur = nxt
                if g == 1:
                    run_casts(1)
                if not full:
                    continue
                tail_pending = tail_gen(g, v, NT, t0, xg, bxg, dst, dstname)
            if tail_pending is not None:
                for _ in tail_pending:
                    pass

        def phase_B(l, srcs, dsts):
            last = (l == DEPTH - 1)
            glist = [(g, grp) for g, grp in enumerate(groups) if not (grp[3] and last)]
            run_casts()
            if not last:
                for dd, tt_ in ((dftc256_d, DF256[0]), (dfts256_d, DF256[1])):
                    S.dma("sp", tt_, dd.rearrange("(n p) k -> p n k", p=128), [], [bDF256], bDF256)

            def dft_gen(idx, out):
                g, (t0, NT, kp0, is_ctx) = glist[idx]
                src, srcname = srcs[1] if is_ctx else srcs[0]
                xg, bxg = load_x(src, srcname, g, t0, NT)
                if is_ctx:
                    nts, kt0 = 2, 0
                else:
                    nts, kt0 = 32, 2
                rF = [bFT[0]] if is_ctx else bFT[1:]
                for nt in range(nts):
                    if is_ctx:
                        dc, ds, bdc, bds, ii = DF256[0], DF256[1], bDF256, bDF256, nt
                    else:
                        ii = nt % 4
                        sl = state["dfslot"] % NDF
                        if ii == 3:
                            state["dfslot"] += 1
                        dc, ds, bdc, bds = DFB[0][sl], DFB[1][sl], bDFB[0][sl], bDFB[1][sl]
                        if ii == 0:
                            n4 = nt // 4
                            S.dma("sp", dc[:], dftc_d[t0 // 512, :, n4 * 4:(n4 + 1) * 4, :], [], [bdc], bdc)
                            S.dma("sp", ds[:], dfts_d[t0 // 512, :, n4 * 4:(n4 + 1) * 4, :], [], [bds], bds)
                    for c in range(2):
                        mm(PS[c][:, 0:NT], FT[:, kt0 + nt, c * 128:(c + 1) * 128], dc[:, ii, 0:NT], nt == 0, nt == nts - 1,
                           rF + [bdc], [bPS[c]], sig=(nt == nts - 1 or ii == 3))
                        mm(PS[2 + c][:, 0:NT], FT[:, kt0 + nt, c * 128:(c + 1) * 128], ds[:, ii, 0:NT], nt == 0, nt == nts - 1,
                           rF + [bds], [bPS[2 + c]], sig=(nt == nts - 1 or ii == 3))
                    yield
                out.append((xg, bxg))

            def tail_gen(idx, xg, bxg):
                g, (t0, NT, kp0, is_ctx) = glist[idx]
                v = 1 if is_ctx else 0
                dst, dstname = dsts[1] if is_ctx else dsts[0]
                for c in range(2):
                    cp("dve", AB[c][:, 0:NT], PS[c][:, 0:NT], [bPS[c]], [bAB[c]])
                    cp("act", AB[2 + c][:, 0:NT], PS[2 + c][:, 0:NT], [bPS[2 + c]], [bAB[2 + c]])
                yield
                cname, sname = ("bdc_ctx", "bds_ctx") if is_ctx else ("bdc_lat", "bds_lat")
                for c in range(2):
                    mm(PS[4 + c][:, 0:NT], CST[cname][:], AB[c][:, 0:NT], True, False, [bCST, bAB[c]], [bPS[4 + c]])
                    mm(PS[4 + c][:, 0:NT], CST[sname][:], AB[2 + c][:, 0:NT], False, True, [bCST, bAB[2 + c]], [bPS[4 + c]], sig=True)
                yield
                for c in range(2):
                    cp("dve" if c == 0 else "act", YT[:, c, 0:NT], PS[4 + c][:, 0:NT], [bPS[4 + c]], [bYT])
                yield
                for o in range(8):
                    pi = 6 + (o % 2)
                    ps, bps = PS[pi], bPS[pi]
                    for c in range(2):
                        mm(ps[:, 0:NT], WOUT[:, 4 + c, o * 128:(o + 1) * 128], YT[:, c, 0:NT], c == 0, c == 1,
                           [bWOUT, bYT], [bps], sig=(c == 1))
                    yield
                    residual(l, 2, v, o, ps, bps, xg, bxg, NT)
                    yield
                store_x(dst, dstname, g, t0, NT, xg, bxg)

            cur = []
            for _ in dft_gen(0, cur):
                pass
            for idx in range(len(glist)):
                xg, bxg = cur[0]
                nxt = []
                chains = [tail_gen(idx, xg, bxg)]
                if idx + 1 < len(glist):
                    chains.append(dft_gen(idx + 1, nxt))
                interleave(chains)
                cur = nxt

        def phase_C(l, srcs, dsts):
            last = (l == DEPTH - 1)
            for i in range(1):
                S.op("pool", lambda e, i=i: e.memset(QM[i][0][64:128, :, :], 0.0), [], [bQM[i]])
                S.op("pool", lambda e, i=i: e.memset(QM[i][1][0:64, :, :], 0.0), [], [bQM[i]])
            pslot = 0
            qtc = 0
            for g, (t0, NT, kp0, is_ctx) in enumerate(groups):
                if is_ctx and last:
                    continue
                v = 1 if is_ctx else 0
                src, srcname = srcs[1] if is_ctx else srcs[0]
                dst, dstname = dsts[1] if is_ctx else dsts[0]
                xg, bxg = load_x(src, srcname, g, t0, NT)
                qi = 0
                S.dma("sp", QM[qi][0][0:64, :, 0:NT], qsl_d(g, 0, 64), [bQD[g]], [bQM[qi]], bQM[qi])
                S.dma("sp", QM[qi][1][64:128, :, 0:NT], qsl_d(g, 64, 128), [bQD[g]], [bQM[qi]], bQM[qi])
                kts = [0, 1] if is_ctx else list(range(NKT))
                rK = [bKT[0]] if is_ctx else bKT
                rV = [bVA[0]] if is_ctx else bVA
                for qt in range(NT // 128):
                    qsl = slice(qt * 128, (qt + 1) * 128)
                    ob = 4 + 2 * (qtc % 2)
                    qtc += 1
                    pend = None
                    nk = len(kts)

                    def pv(p):
                        for (gk, ppt, pkt, pki) in p:
                            mm(PS[ob + gk][:, :], VA[:, pkt, gk * 128:(gk + 1) * 128], PTH[ppt][:, :],
                               pki == 0, pki == nk - 1, rV + [bPTH[ppt]], [bPS[ob + gk]], sig=True)

                    for ki, kt in enumerate(kts):
                        cur = []
                        for gk in range(2):
                            sbank = (pslot % 2) * 2 + gk
                            mm(PS[sbank][:, :], KT[:, kt * 128:(kt + 1) * 128], QM[qi][gk][:, :, qsl], True, True,
                               rK + [bQM[qi]], [bPS[sbank]], sig=True)
                            pt = (pslot % 3) * 2 + gk
                            act(PTH[pt][:, :], PS[sbank][:, :], AF.Exp, [bPS[sbank]], [bPTH[pt]], scale=0.125)
                            cur.append((gk, pt, kt, ki))
                        pslot += 1
                        if pend is not None:
                            pv(pend)
                        pend = cur
                    pv(pend)
                    S.op("dve", lambda e, ob=ob: e.reciprocal(out=RC[0:64, :], in_=PS[ob][64:128, :]), [bPS[ob]], [bRC])
                    S.op("dve", lambda e, ob=ob: e.reciprocal(out=RC[64:128, :], in_=PS[ob + 1][0:64, :]), [bPS[ob + 1]], [bRC])
                    tt("dve", ATT[0:64, :, qsl], PS[ob][0:64, :].rearrange("p (j q) -> p j q", j=4),
                       RC[0:64, :].rearrange("p (j q) -> p j q", j=4), ALU.mult, [bPS[ob], bRC], [bATT])
                    tt("dve", ATT[64:128, :, qsl], PS[ob + 1][64:128, :].rearrange("p (j q) -> p j q", j=4),
                       RC[64:128, :].rearrange("p (j q) -> p j q", j=4), ALU.mult, [bPS[ob + 1], bRC], [bATT])
                for o in range(8):
                    pi = o % 4
                    ps, bps = PS[pi], bPS[pi]
                    for j in range(4):
                        mm(ps[:, 0:NT], WOUT[:, j, o * 128:(o + 1) * 128], ATT[:, j, 0:NT], j == 0, j == 3,
                           [bWOUT, bATT], [bps], sig=(j == 3))
                    residual(l, 2, v, o, ps, bps, xg, bxg, NT)
                store_x(dst, dstname, g, t0, NT, xg, bxg)

        def phase_D(l, srcs, dsts):
            last = (l == DEPTH - 1)
            glist = [(g, grp) for g, grp in enumerate(groups) if not (grp[3] and last)]

            def prep(idx):
                g, (t0, NT, kp0, is_ctx) = glist[idx]
                src, srcname = srcs[1] if is_ctx else srcs[0]
                xg, bxg = load_x(src, srcname, g, t0, NT)
                ht, bht = norm_mod(l, g, xg, bxg, NT, 3, 1 if is_ctx else 0)
                return xg, bxg, ht, bht

            cur = prep(0)
            wj = 0
            for idx, (g, (t0, NT, kp0, is_ctx)) in enumerate(glist):
                v = 1 if is_ctx else 0
                dst, dstname = dsts[1] if is_ctx else dsts[0]
                xg, bxg, ht, bht = cur
                for j in range(8):
                    sl = wj % 2
                    wj += 1
                    S.dma("sp", W1B[sl], w1_bf[l, j].rearrange("p (k n) -> p k n", n=512), [bWC[("w1", l)]], [bW1B[sl]], bW1B[sl])
                    for mi in range(4):
                        m = j * 4 + mi
                        pi = 1 + (m % 4)
                        ps, bps = PS[pi], bPS[pi]
                        for kc in range(8):
                            mm(ps[:, 0:NT], W1B[sl][:, kc, mi * 128:(mi + 1) * 128], ht[:, kc, 0:NT], kc == 0, kc == 7,
                               [bW1B[sl], bht], [bps], sig=(kc == 7))
                        ti = 2 + m % 2
                        act(T_f[ti][:, 0:NT], ps[:, 0:NT], AF.Square, [bps], [bTf[ti]])
                        stt(HID[:, m, 0:NT], ps[:, 0:NT], 0.0, T_f[ti][:, 0:NT], ALU.is_gt, ALU.mult, [bps, bTf[ti]], [bHID])
                nxt = None
                for o in range(8):
                    if o == 4 and idx + 1 < len(glist):
                        nxt = prep(idx + 1)
                    sl = wj % 2
                    wj += 1
                    S.dma("sp", W2B[sl], w2_bf[l, o].rearrange("p (m c) -> p m c", c=128), [bWC[("w2", l)]], [bW2B[sl]], bW2B[sl])
                    pi = 5 + (o % 3)
                    ps, bps = PS[pi], bPS[pi]
                    for m in range(32):
                        mm(ps[:, 0:NT], W2B[sl][:, m, :], HID[:, m, 0:NT], m == 0, m == 31, [bW2B[sl], bHID], [bps], sig=(m == 31))
                    residual(l, 5, v, o, ps, bps, xg, bxg, NT)
                store_x(dst, dstname, g, t0, NT, xg, bxg)
                cur = nxt

        def phase_CD(l, srcs, dsts):
            last = (l == DEPTH - 1)
            glist = [(g, grp) for g, grp in enumerate(groups) if not (grp[3] and last)]
            S.op("pool", lambda e: e.memset(cQM[0][64:128, :, :], 0.0), [], [bcQM])
            S.op("pool", lambda e: e.memset(cQM[1][0:64, :, :], 0.0), [], [bcQM])
            st = {"pslot": 0, "wj1": 0, "wj2": 0}

            def attn_gen(idx, out):
                g, (t0, NT, kp0, is_ctx) = glist[idx]
                v = 1 if is_ctx else 0
                src, srcname = srcs[1] if is_ctx else srcs[0]
                xg, bxg = load_x(src, srcname, g, t0, NT)
                S.dma("sp", cQM[0][0:64, :, 0:NT], qsl_d(g, 0, 64), [bQD[g]], [bcQM], bcQM)
                S.dma("sp", cQM[1][64:128, :, 0:NT], qsl_d(g, 64, 128), [bQD[g]], [bcQM], bcQM)
                kts = [0, 1] if is_ctx else list(range(NKT))
                rK = [bKT[0]] if is_ctx else bKT
                rV = [bVA[0]] if is_ctx else bVA
                nk = len(kts)
                ob = 4
                yield
                for qt in range(NT // 128):
                    qsl = slice(qt * 128, (qt + 1) * 128)
                    pend = None

                    def pv(p):
                        for (gk, ppt, pkt, pki) in p:
                            mm(PS[ob + gk][:, :], VA[:, pkt, gk * 128:(gk + 1) * 128], cPT[ppt][:, :],
                               pki == 0, pki == nk - 1, rV + [bcPT[ppt]], [bPS[ob + gk]], sig=(gk == 1))

                    for ki, kt in enumerate(kts):
                        cur = []
                        for gk in range(2):
                            sbank = (st["pslot"] % 2) * 2 + gk
                            mm(PS[sbank][:, :], KT[:, kt * 128:(kt + 1) * 128], cQM[gk][:, :, qsl], True, True,
                               rK + [bcQM], [bPS[sbank]], sig=True)
                            pt = (st["pslot"] % 2) * 2 + gk
                            act(cPT[pt][:, :], PS[sbank][:, :], AF.Exp, [bPS[sbank]], [bcPT[pt]], scale=0.125)
                            cur.append((gk, pt, kt, ki))
                        st["pslot"] += 1
                        if pend is not None:
                            pv(pend)
                        pend = cur
                        yield
                    pv(pend)
                    cp("dve", GV[:, :], PS[ob][:, :], [bPS[ob]], [bGV])
                    cp("dve", GV2[:, :], PS[ob + 1][:, :], [bPS[ob + 1]], [bGV2])
                    yield
                    S.op("dve", lambda e: e.reciprocal(out=RC[0:64, :], in_=GV[64:128, :]), [bGV], [bRC])
                    tt("dve", cATT[0:64, :, qsl], GV[0:64, :].rearrange("p (j q) -> p j q", j=4),
                       RC[0:64, :].rearrange("p (j q) -> p j q", j=4), ALU.mult, [bGV, bRC], [bcATT])
                    yield
                    S.op("dve", lambda e: e.reciprocal(out=RC[64:128, :], in_=GV2[0:64, :]), [bGV2], [bRC])
                    tt("dve", cATT[64:128, :, qsl], GV2[64:128, :].rearrange("p (j q) -> p j q", j=4),
                       RC[64:128, :].rearrange("p (j q) -> p j q", j=4), ALU.mult, [bGV2, bRC], [bcATT])
                    yield
                for o in range(8):
                    pi = o % 4
                    ps, bps = PS[pi], bPS[pi]
                    for j in range(4):
                        mm(ps[:, 0:NT], WOUT[:, j, o * 128:(o + 1) * 128], cATT[:, j, 0:NT], j == 0, j == 3,
                           [bWOUT, bcATT], [bps], sig=(j == 3))
                    residual(l, 2, v, o, ps, bps, xg, bxg, NT)
                    yield
                out.append((xg, bxg))

            def mlp_gen(idx, xg, bxg):
                g, (t0, NT, kp0, is_ctx) = glist[idx]
                v = 1 if is_ctx else 0
                dst, dstname = dsts[1] if is_ctx else dsts[0]
                res = []
                for _ in norm_gen(l, xg, bxg, NT, 3, v, res, ssb=6):
                    yield
                ht, bht = res[0]
                for j in range(8):
                    sl = st["wj1"] % 2
                    st["wj1"] += 1
                    S.dma("sp", cW1B[sl], w1_bf[l, j].rearrange("p (k n) -> p k n", n=512), [bWC[("w1", l)]], [bcW1B[sl]], bcW1B[sl])
                    for mi in range(4):
                        m = j * 4 + mi
                        pi = 6 + (m % 2)
                        ps, bps = PS[pi], bPS[pi]
                        for kc in range(8):
                            mm(ps[:, 0:NT], cW1B[sl][:, kc, mi * 128:(mi + 1) * 128], ht[:, kc, 0:NT], kc == 0, kc == 7,
                               [bcW1B[sl], bht], [bps], sig=(kc == 7))
                            if kc == 3:
                                yield
                        ti = 2 + m % 2
                        ts("dve", T_f[ti][:, 0:NT], ps[:, 0:NT], 0.0, None, ALU.max, None, [bps], [bTf[ti]])
                        tt("pool", cHID[:, m, 0:NT], T_f[ti][:, 0:NT], T_f[ti][:, 0:NT], ALU.mult, [bTf[ti]], [bcHID])
                        yield
                for o in range(8):
                    sl = st["wj2"] % 2
                    st["wj2"] += 1
                    S.dma("sp", cW2B[sl], w2_bf[l, o].rearrange("p (m c) -> p m c", c=128), [bWC[("w2", l)]], [bcW2B[sl]], bcW2B[sl])
                    pi = 6 + (o % 2)
                    ps, bps = PS[pi], bPS[pi]
                    for m in range(32):
                        mm(ps[:, 0:NT], cW2B[sl][:, m, :], cHID[:, m, 0:NT], m == 0, m == 31, [bcW2B[sl], bcHID], [bps], sig=(m == 31))
                        if m % 4 == 3:
                            yield
                    residual(l, 5, v, o, ps, bps, xg, bxg, NT)
                store_x(dst, dstname, g, t0, NT, xg, bxg)

            cur = []
            for _ in attn_gen(0, cur):
                pass
            for idx in range(len(glist)):
                xg, bxg = cur[0]
                nxt = []
                chains = [mlp_gen(idx, xg, bxg)]
                if idx + 1 < len(glist):
                    chains.insert(0, attn_gen(idx + 1, nxt))
                interleave(chains)
                cur = nxt

        S.barrier()
        if "nocast" not in debug:
            cast_queue.extend([(0, "a"), (0, "b")] + ([(1, "a"), (1, "b")] if nlayers > 1 else []))
            run_casts(1)
        stop_after = None
        for d_ in debug:
            if d_.startswith("stop:"):
                stop_after = d_[5:]
        cur_x, cur_c = (xT, "xT"), (ctxT, "ctxT")
        done = False
        for l in range(nlayers):
            if stop_after == "pro" or done:
                done = True
                break
            last = (l == DEPTH - 1)
            if "unmerged" in debug:
                seq = [("A", phase_A), ("B", phase_B), ("C", phase_C), ("D", phase_D)]
            else:
                seq = [("A", phase_A), ("B", phase_B), ("D", phase_CD)]
            for pi_, (pn, fn) in enumerate(seq):
                S.barrier()
                nx = (xb, "xb") if cur_x[1] == "xa" else (xa, "xa")
                ncx = (cb, "cb") if cur_c[1] == "ca" else (ca, "ca")
                if pn == "D" and l == nlayers - 1:
                    nx = (outT, "outT")
                try:
                    fn(l, (cur_x, cur_c), (nx, ncx))
                except _Stop:
                    done = True
                    cur_x = (outT, "outT")
                    break
                cur_x = nx
                if not last:
                    cur_c = ncx
                if stop_after == f"{l}{pn}":
                    done = True
                    break
            if done:
                break
        if done and cur_x[1] != "outT":
            for g in range(1, 9):
                t0 = (g - 1) * 512
                xg, bxg = load_x(cur_x[0], cur_x[1], g, t0, 512)
                store_x(outT, "outT", g, t0, 512, xg, bxg)
        if "ctxout" in debug:
            d = nc.dram_tensor("dbg_ctxout", [128, 8, CTX], F32, kind="ExternalOutput").ap()
            dbg_out["ctxout"] = d
            bXD[("dbgc", 0)] = B("dbgc0")
            xg, bxg = load_x(cur_c[0], cur_c[1], 0, 0, CTX)
            store_x(d, "dbgc", 0, 0, CTX, xg, bxg)
        if "kt" in debug:
            dbg_dump("kt", KT, [128, NTOK], BF16, bKT)
        if "va" in debug:
            dbg_dump("va", VA, [128, NKT, 256], BF16, bVA)
        if "ft" in debug:
            dbg_dump("ft", FT, [128, NKT, 256], BF16, bFT)
        if "qT" in debug:
            d = nc.dram_tensor("dbg_qT", [128, 4, NTOK], BF16, kind="ExternalOutput").ap()
            dbg_out["qT"] = d
            for i in range(2):
                S.dma("sp", QST[i][:, :, :], qsl_d(i + 1), bQD, [bQST[i]], bQST[i])
                S.dma("sp", d[:, :, 256 + i * 512:256 + (i + 1) * 512], QST[i][:, :, :], [bQST[i]], [], bQST[i])
        S.barrier()
        S.emit()
    return nc


_NC_CACHE = {}


def kernel(**inputs):
    inputs = {k: np.asarray(v) for k, v in inputs.items()}
    sh, cores = host_layout(inputs)
    if "nc" not in _NC_CACHE:
        _NC_CACHE["nc"] = build()
    nc = _NC_CACHE["nc"]
    in_maps = []
    for b in range(8):
        m = dict(sh)
        m.update(cores[b])
        in_maps.append(m)
    res = run_bass_kernel_spmd(nc, in_maps, core_ids=list(range(8)))
    out = np.stack([np.ascontiguousarray(np.asarray(res.results[b]["outT"]).transpose(0, 3, 2, 1).reshape(S_LAT, D))
                    for b in range(8)], axis=0)
    return out.astype(np.float32)
```
